# Optimizing a Trainium2 kernel written in Bass

```python
import math
import jax, jax.numpy as jnp
from jax import lax
import numpy as np

D_MODEL = 1024
BATCH = 8
SEQ = 2048
DEPTH = 1

CHUNK = 64
CONV_DIM = D_MODEL // 2
CONV_WIDTH = 31
N_HEADS = 8
HEAD_DIM = 64
ATT_DIM = N_HEADS * HEAD_DIM
QBLOCK = 128
D_FF = ((8 * D_MODEL + 3 * 256 - 1) // (3 * 256)) * 256
EPS = 1e-6
IN_COLS = 2 * CONV_DIM + 3 * ATT_DIM + 2 * D_MODEL

kernel_name = "hybrid_conformer_conv_stickbreaking_block"


def rmsnorm(x, g):
    xf = x.astype(jnp.float32)
    y = xf * lax.rsqrt(jnp.mean(xf * xf, axis=-1, keepdims=True) + EPS)
    return (y * g.astype(jnp.float32)).astype(x.dtype)


def layernorm(x, g, b):
    xf = x.astype(jnp.float32)
    mu = jnp.mean(xf, axis=-1, keepdims=True)
    var = jnp.mean(jnp.square(xf - mu), axis=-1, keepdims=True)
    y = (xf - mu) * lax.rsqrt(var + EPS)
    return (y * g.astype(jnp.float32) + b.astype(jnp.float32)).astype(x.dtype)


def causal_depthwise_conv(u, w, b):
    k, c = w.shape
    up = jnp.pad(u, ((0, 0), (k - 1, 0), (0, 0)))
    y = lax.conv_general_dilated(
        up, w[:, None, :].astype(u.dtype), window_strides=(1,), padding='VALID',
        dimension_numbers=('NWC', 'WIO', 'NWC'), feature_group_count=c)
    return y + b.astype(u.dtype)


def stick_breaking_attention(q, k, v):
    s_len = q.shape[1]
    scale = 1.0 / math.sqrt(q.shape[-1])
    outs = []
    for i in range(s_len // QBLOCK):
        q0 = i * QBLOCK
        q1 = q0 + QBLOCK
        qb = q[:, q0:q1].astype(jnp.float32)
        kb = k[:, :q1].astype(jnp.float32)
        vb = v[:, :q1].astype(jnp.float32)
        z = jnp.einsum('bqhd,bkhd->bhqk', qb, kb) * scale
        t_idx = q0 + jnp.arange(QBLOCK)[:, None]
        s_idx = jnp.arange(q1)[None, :]
        mask = s_idx < t_idx
        log_1m = jnp.where(mask, jax.nn.log_sigmoid(-z), 0.0)
        rev_excl = lax.cumsum(log_1m, axis=3, reverse=True) - log_1m
        a = jnp.where(mask, jnp.exp(jax.nn.log_sigmoid(z) + rev_excl), 0.0)
        outs.append(jnp.einsum('bhqk,bkhd->bqhd', a, vb))
    return jnp.concatenate(outs, axis=1).astype(q.dtype)


def setup_inputs(seed: int = 0) -> dict:
    key = jax.random.key(seed)
    ks = jax.random.split(key, 20)
    f32 = jnp.float32

    def w(k, shape, fan_in):
        return jax.random.normal(k, shape, f32) * (fan_in ** -0.5)

    def gain(k, n):
        return 1.0 + 0.05 * jax.random.normal(k, (n,), f32)

    def bias(k, n):
        return 0.02 * jax.random.normal(k, (n,), f32)

    return {
        'x': jax.random.normal(ks[0], (BATCH, SEQ, D_MODEL), f32),
        'norm_mix_pre': gain(ks[1], D_MODEL),
        'w_in': w(ks[2], (D_MODEL, IN_COLS), D_MODEL),
        'conv_dw_w': w(ks[3], (CONV_WIDTH, CONV_DIM), CONV_WIDTH),
        'conv_dw_b': bias(ks[4], CONV_DIM),
        'conv_ln_g': gain(ks[5], CONV_DIM),
        'conv_ln_b': bias(ks[6], CONV_DIM),
        'w_conv_branch': w(ks[7], (CONV_DIM, D_MODEL), CONV_DIM),
        'b_conv_branch': bias(ks[8], D_MODEL),
        'w_att_branch': w(ks[9], (ATT_DIM, D_MODEL), ATT_DIM),
        'w_out': w(ks[10], (D_MODEL, D_MODEL), D_MODEL),
        'norm_mix_post': gain(ks[11], D_MODEL),
        'norm_ffn_pre': gain(ks[12], D_MODEL),
        'w_ffn_up': w(ks[13], (D_MODEL, 2 * D_FF), D_MODEL),
        'w_ffn_down': w(ks[14], (D_FF, D_MODEL), D_FF),
        'norm_ffn_post': gain(ks[15], D_MODEL),
    }


def reference(x, norm_mix_pre, w_in, conv_dw_w, conv_dw_b, conv_ln_g, conv_ln_b,
              w_conv_branch, b_conv_branch, w_att_branch, w_out, norm_mix_post,
              norm_ffn_pre, w_ffn_up, w_ffn_down, norm_ffn_post):
    b, s, d = x.shape
    for _ in range(DEPTH):
        h = rmsnorm(x, norm_mix_pre)
        proj = jnp.einsum('bsd,de->bse', h, w_in)
        splits = np.cumsum([2 * CONV_DIM, ATT_DIM, ATT_DIM, ATT_DIM, D_MODEL]).tolist()
        conv_in, q, k, v, g_conv, g_att = jnp.split(proj, splits, axis=-1)

        u = jax.nn.glu(conv_in, axis=-1)
        u = causal_depthwise_conv(u, conv_dw_w, conv_dw_b)
        u = jax.nn.silu(layernorm(u, conv_ln_g, conv_ln_b))
        conv_out = jnp.einsum('bsc,cd->bsd', u, w_conv_branch) + b_conv_branch

        q = q.reshape(b, s, N_HEADS, HEAD_DIM)
        k = k.reshape(b, s, N_HEADS, HEAD_DIM)
        v = v.reshape(b, s, N_HEADS, HEAD_DIM)
        att = stick_breaking_attention(q, k, v).reshape(b, s, ATT_DIM)
        att_out = jnp.einsum('bsc,cd->bsd', att, w_att_branch)

        merged = jax.nn.sigmoid(g_conv) * conv_out + jax.nn.sigmoid(g_att) * att_out
        mix = jnp.einsum('bsd,de->bse', merged, w_out)
        x = x + rmsnorm(mix, norm_mix_post)

        h = rmsnorm(x, norm_ffn_pre)
        gu = jnp.einsum('bsd,df->bsf', h, w_ffn_up)
        gate, up = jnp.split(gu, 2, axis=-1)
        ff = jnp.einsum('bsf,fd->bsd', jax.nn.silu(gate) * up, w_ffn_down)
        x = x + rmsnorm(ff, norm_ffn_post)
    return x
```

```python
import contextlib
import numpy as np
import concourse.bass as bass
import concourse.mybir as mybir
from concourse.bass_utils import run_bass_kernel_spmd

F32 = mybir.dt.float32
BF16 = mybir.dt.bfloat16
AF = mybir.ActivationFunctionType
ALU = mybir.AluOpType

D = 1024
KD = 8
CONV = 512
KW = 31
PAD = KW - 1
NH = 8
DH = 64
DFF = 2816
KF = 22
INC = 4608
EPS = 1e-6
NEG = -30000.0


class Op:
    __slots__ = ("eng", "fn", "deps", "signal", "sem", "val", "dma", "idx")

    def __init__(self, eng, fn, dma):
        self.eng = eng
        self.fn = fn
        self.dma = dma
        self.deps = set()
        self.signal = False
        self.sem = None
        self.val = 0


class Sched:
    ENG = ("pe", "act", "dve", "pool", "sp")

    def __init__(self):
        self.ops = {e: [] for e in self.ENG}
        self.last_w = {}
        self.readers = {}
        self.n = 0
        self.pending = {e: set() for e in self.ENG}

    def add(self, eng, fn, reads=(), writes=(), dma=None):
        op = Op(eng, fn, dma)
        op.idx = self.n
        self.n += 1
        deps = set(self.pending[eng])
        self.pending[eng] = set()
        for k in reads:
            w = self.last_w.get(k)
            if w is not None:
                deps.add(w)
            if isinstance(k, tuple) and k and k[0] == "ps":
                deps.update(r for r in self.readers.get(k, ()) if r.eng != eng)
        for k in writes:
            w = self.last_w.get(k)
            if w is not None:
                deps.add(w)
            deps.update(self.readers.get(k, ()))
        if eng == "pe":
            deps = {d for d in deps if not (d.eng == "pe" and d.dma is None)}
        newest = {}
        rest = set()
        for d in deps:
            if d.dma is None and d.eng in ("pe", "act", "dve"):
                if d.eng not in newest or newest[d.eng].idx < d.idx:
                    newest[d.eng] = d
            else:
                rest.add(d)
        deps = rest | set(newest.values())
        op.deps = deps
        for d in deps:
            d.signal = True
        for k in reads:
            self.readers.setdefault(k, []).append(op)
        for k in writes:
            self.last_w[k] = op
            self.readers[k] = []
        self.ops[eng].append(op)
        return op

    def barrier(self):
        lasts = set()
        for e in self.ENG:
            if self.ops[e]:
                lasts.add(self.ops[e][-1])
            for o in self.ops[e]:
                if o.dma is not None:
                    lasts.add(o)
        newest = {}
        keep = set()
        for o in lasts:
            if o.dma is None:
                keep.add(o)
            else:
                if o.dma not in newest or newest[o.dma].idx < o.idx:
                    newest[o.dma] = o
        keep.update(newest.values())
        for e in self.ENG:
            self.pending[e] = set(keep)

    def emit(self, nc, final_waits=()):
        with contextlib.ExitStack() as es:
            esem = {e: es.enter_context(nc.semaphore("s_" + e)) for e in self.ENG}
            dsem = {}
            dcnt = {}
            allops = []
            for e in self.ENG:
                allops.extend(self.ops[e])
            allops.sort(key=lambda o: o.idx)
            for op in allops:
                if op.dma is not None:
                    if op.dma not in dsem:
                        dsem[op.dma] = es.enter_context(nc.semaphore("d_%d" % len(dsem)))
                        dcnt[op.dma] = 0
                    dcnt[op.dma] += 16
                    op.sem = dsem[op.dma]
                    op.val = dcnt[op.dma]
            for e in self.ENG:
                c = 0
                for op in self.ops[e]:
                    if op.dma is None and op.signal:
                        c += 1
                        op.sem = esem[e]
                        op.val = c
            sched = self

            def run(eng_name, eng):
                waited = {}
                for op in sched.ops[eng_name]:
                    for d in sorted(op.deps, key=lambda o: o.idx):
                        sid = id(d.sem)
                        if waited.get(sid, 0) >= d.val:
                            continue
                        eng.wait_ge(d.sem, d.val)
                        waited[sid] = d.val
                    ins = op.fn(eng)
                    if op.dma is not None:
                        ins.then_inc(op.sem, 16)
                    elif op.signal:
                        ins.then_inc(op.sem, 1)
                if eng_name == "sp":
                    done = {}
                    for o in final_waits:
                        if done.get(id(o.sem), (None, 0))[1] < o.val:
                            done[id(o.sem)] = (o.sem, o.val)
                    for sem, val in done.values():
                        eng.wait_ge(sem, val)

            with nc.Block() as block:

                @block.tensor
                def _(e):
                    run("pe", e)

                @block.scalar
                def _(e):
                    run("act", e)

                @block.vector
                def _(e):
                    run("dve", e)

                @block.gpsimd
                def _(e):
                    run("pool", e)

                @block.sync
                def _(e):
                    run("sp", e)


class Buf:
    def __init__(self, t, w):
        self.t = t
        self.w = w

    def v(self, off, dims, p0=0, npart=128):
        return bass.AP(self.t, p0 * self.w + off, [[self.w, npart]] + [[s, c] for (s, c) in dims])


def build(S, dbg=False):
    NT = S // 512
    NB = S // 128
    nc = bass.Bass("TRN2", target_bir_lowering=False)
    dram = lambda name, shape: nc.dram_tensor(name, shape, F32, kind="ExternalInput")
    x_h = dram("x", [S, D])
    g1_h = dram("norm_mix_pre", [D])
    win_h = dram("w_in", [D, INC])
    cw_h = dram("conv_dw_w", [KW, CONV])
    cb_h = dram("conv_dw_b", [CONV])
    lg_h = dram("conv_ln_g", [CONV])
    lb_h = dram("conv_ln_b", [CONV])
    wc_h = dram("w_conv_branch", [CONV, D])
    bc_h = dram("b_conv_branch", [D])
    wa_h = dram("w_att_branch", [CONV, D])
    wo_h = dram("w_out", [D, D])
    g2_h = dram("norm_mix_post", [D])
    g3_h = dram("norm_ffn_pre", [D])
    wu_h = dram("w_ffn_up", [D, 2 * DFF])
    wd_h = dram("w_ffn_down", [DFF, D])
    g4_h = dram("norm_ffn_post", [D])
    cst_h = dram("consts", [128, 640])
    out_h = nc.dram_tensor("out", [S, D], F32, kind="ExternalOutput")

    es = contextlib.ExitStack()
    with es:
        def sb(name, w, dt):
            return Buf(es.enter_context(nc.sbuf_tensor(name, [128, w], dt)), w)

        US = PAD + S
        UREG = max(4 * US, 8192)
        A1W = UREG + 12 * S
        ar1 = es.enter_context(nc.sbuf_tensor("ar1", [128, A1W], BF16))
        A1 = Buf(ar1, A1W)
        A1F = Buf(ar1.bitcast(F32), A1W // 2)
        A2W = 16 * S
        ar2 = es.enter_context(nc.sbuf_tensor("ar2", [128, A2W], BF16))
        A2 = Buf(ar2, A2W)
        A2F = Buf(ar2.bitcast(F32), A2W // 2)
        A3W = 20480
        ar3 = es.enter_context(nc.sbuf_tensor("ar3", [128, A3W], BF16))
        A3 = Buf(ar3, A3W)
        A3F = Buf(ar3.bitcast(F32), A3W // 2)
        wst = [sb("wst%d" % i, 2048, F32) for i in range(2)]
        wbf = [sb("wbf%d" % i, 2048, BF16) for i in range(2)]
        cstf = sb("cstf", 640, F32)
        cstb = sb("cstb", 640, BF16)
        cols = sb("cols", 64, F32)
        stat = sb("stat", 5 * NB + 8, F32)
        cwb = sb("cwb", CONV, BF16)
        cwT = sb("cwT", 128, F32)
        g13 = sb("g13", D, F32)

        C_G1, C_G3, C_CB, C_LG, C_LB, C_BC = 0, 8, 16, 20, 24, 28
        uT_off, qT_off, kT_off, v_off = 0, UREG, UREG + 4 * S, UREG + 8 * S
        hT_off, aT_off, u2_off = 0, 8 * S, 12 * S
        IDN, TRI, ONE, MSK, ONM = 0, 128, 256, 384, 512

        XN_O, JUNK_O = 16384, 17408
        XN = A3.v(XN_O, [(1, D)])
        JUNK = A3.v(JUNK_O, [(1, D)])

        ps = [Buf(es.enter_context(nc.psum_tensor("ps%d" % i, [128, 1024], F32)), 1024) for i in range(4)]
        psb = [Buf(p.t.bitcast(BF16), 2048) for p in ps]

        def bank(i):
            return ps[i // 2], (i % 2) * 512

        Sd = Sched()
        add = Sd.add

        add("sp", lambda e: e.dma_start(out=cstf.v(0, [(1, 640)]), in_=cst_h.ap()), writes=["cstf"], dma="c0")
        add("dve", lambda e: e.tensor_copy(out=cstb.v(0, [(1, 640)]), in_=cstf.v(0, [(1, 640)])), reads=["cstf"], writes=["cstb"])
        add("dve", lambda e: e.memset(stat.v(0, [(1, 5 * NB + 8)]), 0.0), writes=[("stat", i) for i in range(5 * NB)])

        def colvec(h, n, slot):
            add("sp", lambda e: e.dma_start(out=cols.v(slot, [(1, n)]), in_=h.ap().rearrange("(j p) -> p j", p=128),
                                            allow_slow_non_contiguous=True),
                writes=[("cols", slot)], dma=("cols", slot))

        colvec(cb_h, 4, C_CB)
        colvec(lg_h, 4, C_LG)
        colvec(lb_h, 4, C_LB)
        colvec(bc_h, 8, C_BC)

        def bcast_row(h):
            return bass.AP(h, 0, [[0, 128], [1, D]])

        G13 = g13.v(0, [(1, D)])
        add("sp", lambda e: e.dma_start(out=G13, in_=bcast_row(g1_h)), writes=["g13"], dma="g13")

        ident = lambda: cstb.v(IDN, [(1, 128)])
        trineg = lambda: cstb.v(TRI, [(1, 128)])
        onesneg = lambda: cstb.v(ONE, [(1, 128)])
        maskneg = lambda: cstb.v(MSK, [(1, 128)])
        onesmean = lambda: cstb.v(ONM, [(1, 128)])

        wslot = [0]
        ncast = [0]

        def load_w(srcs, kch, ncols, gslot=None, cast_eng=None):
            s = wslot[0] % 2
            wslot[0] += 1
            c0 = 0
            pkeys = []
            for pi, src in enumerate(srcs):
                n_i = src.shape[-1]
                add("sp", (lambda s=s, c0=c0, n_i=n_i, src=src: lambda e: e.dma_start(
                    out=wst[s].v(c0, [(ncols, kch), (1, n_i)]), in_=src))(),
                    writes=[("wst", s, pi)], dma=("wst", s, pi))
                pkeys.append(("wst", s, pi))
                c0 += n_i
            allk = [("wst", s, 0), ("wst", s, 1)]
            if gslot is None:
                ce = cast_eng or ("act" if (ncast[0] % 2 == 0) else "dve")
                ncast[0] += 1
                if ce == "act":
                    add("act", lambda e, s=s: e.copy(out=wbf[s].v(0, [(1, kch * ncols)]), in_=wst[s].v(0, [(1, kch * ncols)])),
                        reads=allk, writes=[("wbf", s)])
                else:
                    add("dve", lambda e, s=s: e.tensor_copy(out=wbf[s].v(0, [(1, kch * ncols)]), in_=wst[s].v(0, [(1, kch * ncols)])),
                        reads=allk, writes=[("wbf", s)])
            else:
                for k in range(kch):
                    add("dve", lambda e, s=s, k=k: e.tensor_scalar(
                        out=wbf[s].v(k * ncols, [(1, ncols)]), in0=wst[s].v(k * ncols, [(1, ncols)]),
                        scalar1=cols.v(gslot + k, [(1, 1)]), scalar2=None, op0=ALU.mult),
                        reads=allk + [("cols", gslot)], writes=[("wbf", s)])
            return s

        def load_w_dma(srcs, kch, ncols):
            s_ = wslot[0] % 2
            wslot[0] += 1
            c0 = 0
            for pi, src in enumerate(srcs):
                n_i = src.shape[-1]
                add("sp", (lambda s_=s_, c0=c0, n_i=n_i, src=src: lambda e: e.dma_start(
                    out=wst[s_].v(c0, [(ncols, kch), (1, n_i)]), in_=src))(),
                    writes=[("wst", s_, pi)], dma=("wst", s_, pi))
                c0 += n_i
            return s_

        def load_w_cast(s_, kch, ncols):
            add("dve", lambda e: e.tensor_copy(out=wbf[s_].v(0, [(1, kch * ncols)]), in_=wst[s_].v(0, [(1, kch * ncols)])),
                reads=[("wst", s_, 0), ("wst", s_, 1)], writes=[("wbf", s_)])

        def run_stream(items, pre=None, after_first=None):
            nxt = items[0][0]() if pre is None else pre
            for i, (ld, body) in enumerate(items):
                cur = nxt
                if i + 1 < len(items):
                    nxt = items[i + 1][0]()
                body(cur)
                if i == 0 and after_first is not None:
                    after_first()

        def wrows(h, c0, n):
            return h.ap().rearrange("(k p) c -> p k c", p=128)[:, :, c0:c0 + n]

        def mm(out, lhsT, rhs, start, stop, reads, writes):
            add("pe", lambda e: e.matmul(out, lhsT=lhsT, rhs=rhs, start=start, stop=stop), reads=reads, writes=writes)

        def rstd_from(ap, scale, key):
            add("act", lambda e: e.activation(out=ap, in_=ap, func=AF.Ln, scale=scale, bias=EPS), reads=[key], writes=[key])
            add("act", lambda e: e.activation(out=ap, in_=ap, func=AF.Exp, scale=-0.5), reads=[key], writes=[key])

        XN2_O = 15360
        XNB = [(XN_O, "xn0"), (XN2_O, "xn1")]

        def norm_transpose_seq(blocks, pbase=6):
            def s1(i):
                pre, src_fn, src_key, scol, dst_ap, dst_key = blocks[i]
                if pre is not None:
                    pre()
                skey = ("stat", scol)
                sap = stat.v(scol, [(1, 1)])
                add("act", lambda e: e.activation(out=JUNK, in_=src_fn(), func=AF.Square, accum_out=sap),
                    reads=[src_key, skey], writes=["junk", skey])
                rstd_from(sap, 1.0 / D, skey)

            def s2(i):
                pre, src_fn, src_key, scol, dst_ap, dst_key = blocks[i]
                skey = ("stat", scol)
                sap = stat.v(scol, [(1, 1)])
                xo, xk = XNB[i % 2]
                pb = pbase + (i % 2)
                add("dve", lambda e: e.scalar_tensor_tensor(out=A3.v(xo, [(1, D)]), in0=src_fn(), scalar=sap, in1=G13, op0=ALU.mult, op1=ALU.mult),
                    reads=[src_key, skey, "g13"], writes=[xk])
                PB = psb[pb // 2]
                ob = (pb % 2) * 1024
                for k in range(KD):
                    add("pe", lambda e, k=k: e.transpose(out=PB.v(ob + k * 128, [(1, 128)]),
                                                         in_=A3.v(xo + k * 128, [(1, 128)]), identity=ident()),
                        reads=[xk, "cstb"], writes=[("ps", pb)])

            def s3(i):
                pre, src_fn, src_key, scol, dst_ap, dst_key = blocks[i]
                pb = pbase + (i % 2)
                PB = psb[pb // 2]
                ob = (pb % 2) * 1024
                add("dve", lambda e: e.tensor_copy(out=dst_ap, in_=PB.v(ob, [(128, KD), (1, 128)])),
                    reads=[("ps", pb)], writes=[dst_key])

            s1(0)
            for i in range(len(blocks)):
                if i + 1 < len(blocks):
                    s1(i + 1)
                s2(i)
                if i >= 1:
                    s3(i - 1)
            s3(len(blocks) - 1)

        XS = [A3F.v(0, [(1, D)]), A3F.v(D, [(1, D)]), A3F.v(3072, [(1, D)]), A3F.v(4096, [(1, D)])]

        def xload(b):
            s_ = b % 4
            return lambda: add("sp", lambda e: e.dma_start(out=XS[s_], in_=x_h.ap()[b * 128:(b + 1) * 128, :]),
                               writes=[("xs", s_)], dma=("xs", s_))
        norm_transpose_seq([(xload(b), (lambda b=b: XS[b % 4]), ("xs", b % 4), b,
                             A2.v(hT_off + b * 128, [(S, KD), (1, 128)]), ("hT", b // 4)) for b in range(NB)])

        add("dve", lambda e: e.memset(A1.v(uT_off, [(US, 4), (1, PAD)]), 0.0), writes=[("uT", -1)])
        SG = A3F.v(2048, [(1, 512)])
        def a1_body(j):
            def body(s):
                for t in range(NT):
                    ba, bg = 2 * (t % 2), 2 * (t % 2) + 1
                    Pa, oa = bank(ba)
                    Pg, og = bank(bg)
                    for k in range(KD):
                        mm(Pa.v(oa, [(1, 512)]), wbf[s].v(k * 256, [(1, 128)]), A2.v(hT_off + k * S + t * 512, [(1, 512)]),
                           k == 0, k == KD - 1, [("wbf", s), ("hT", t)], [("ps", ba)])
                    for k in range(KD):
                        mm(Pg.v(og, [(1, 512)]), wbf[s].v(k * 256 + 128, [(1, 128)]), A2.v(hT_off + k * S + t * 512, [(1, 512)]),
                           k == 0, k == KD - 1, [("wbf", s), ("hT", t)], [("ps", bg)])
                    add("act", lambda e, Pg=Pg, og=og: e.activation(out=SG, in_=Pg.v(og, [(1, 512)]), func=AF.Sigmoid),
                        reads=[("ps", bg)], writes=["sg"])
                    add("dve", lambda e, Pa=Pa, oa=oa, t=t: e.tensor_tensor(
                        out=A1.v(uT_off + j * US + PAD + t * 512, [(1, 512)]), in0=Pa.v(oa, [(1, 512)]), in1=SG, op=ALU.mult),
                        reads=[("ps", ba), "sg"], writes=[("uT", j, t)])
            return body
        run_stream([((lambda j=j: load_w([wrows(win_h, 128 * j, 128), wrows(win_h, 512 + 128 * j, 128)], KD, 256)), a1_body(j))
                    for j in range(4)])

        cwf = A3F.v(2560, [(1, CONV)], npart=KW)
        add("sp", lambda e: e.dma_start(out=cwf, in_=cw_h.ap()), writes=["cwf"], dma="cw")
        add("dve", lambda e: e.tensor_copy(out=cwb.v(0, [(1, CONV)], npart=KW), in_=cwf), reads=["cwf"], writes=["cwb"])
        for j in range(4):
            add("pe", lambda e, j=j: e.transpose(out=psb[3].v(j * 32, [(1, KW)]), in_=cwb.v(j * 128, [(1, 128)], npart=KW),
                                                 identity=cstb.v(IDN, [(1, KW)], npart=KW)),
                reads=["cwb", "cstb"], writes=[("ps", 6)])
        add("dve", lambda e: e.tensor_copy(out=cwT.v(0, [(32, 4), (1, KW)]), in_=psb[3].v(0, [(32, 4), (1, KW)])),
            reads=[("ps", 6)], writes=["cwT"])
        Y_off = qT_off // 2
        YRING = (NT > 2)
        assert (not YRING) or 8 * 512 * 2 <= 4 * S

        def Yv(j, t):
            if YRING:
                return A1F.v(kT_off // 2 + ((t % 2) * 4 + j) * 512, [(1, 512)])
            return A1F.v(Y_off + j * S + t * 512, [(1, 512)])
        ykey = lambda j, t: ("y", j, (t % 2) if YRING else t)
        DG = lambda k: A3.v(6144 + k * 128, [(1, 128)])
        YB = lambda j: A3.v(10112 + j * 512, [(1, 512)])
        YQ = lambda j: A3.v(12160 + j * 512, [(1, 512)])
        MU = A3F.v(7104, [(1, 512)])
        RS = A3F.v(7616, [(1, 512)])
        T1 = A3F.v(9216, [(1, 512)])
        def ln_tile(t):
            for j in range(4):
                add("dve", lambda e, j=j, t=t: e.tensor_copy(out=YB(j), in_=Yv(j, t)),
                    reads=[ykey(j, t)], writes=[("yb", j)])
                add("act", lambda e, j=j, t=t: e.activation(out=YQ(j), in_=Yv(j, t), func=AF.Square),
                    reads=[ykey(j, t)], writes=[("yq", j)])
            Pm, om = bank(4)
            Pq, oq = bank(5)
            for j in range(4):
                mm(Pm.v(om, [(1, 512)]), onesmean(), YB(j), j == 0, j == 3, ["cstb", ("yb", j)], [("ps", 4)])
            for j in range(4):
                mm(Pq.v(oq, [(1, 512)]), onesmean(), YQ(j), j == 0, j == 3, ["cstb", ("yq", j)], [("ps", 5)])
            add("act", lambda e, Pm=Pm, om=om: e.copy(out=MU, in_=Pm.v(om, [(1, 512)])), reads=[("ps", 4)], writes=["mu"])
            add("dve", lambda e: e.tensor_tensor(out=T1, in0=MU, in1=MU, op=ALU.mult), reads=["mu"], writes=["t1"])
            add("dve", lambda e, Pq=Pq, oq=oq: e.tensor_tensor(out=RS, in0=Pq.v(oq, [(1, 512)]), in1=T1, op=ALU.subtract),
                reads=[("ps", 5), "t1"], writes=["rs"])
            rstd_from(RS, 1.0, "rs")
            for j in range(4):
                add("dve", lambda e, j=j, t=t: e.tensor_tensor(out=T1, in0=Yv(j, t), in1=MU, op=ALU.subtract),
                    reads=[ykey(j, t), "mu"], writes=["t1"])
                add("dve", lambda e: e.tensor_tensor(out=T1, in0=T1, in1=RS, op=ALU.mult), reads=["t1", "rs"], writes=["t1"])
                add("act", lambda e, j=j, t=t: e.activation(out=A2.v(u2_off + j * S + t * 512, [(1, 512)]), in_=T1, func=AF.Silu,
                                                            scale=cols.v(C_LG + j, [(1, 1)]), bias=cols.v(C_LB + j, [(1, 1)])),
                    reads=["t1", ("cols", C_LG), ("cols", C_LB)], writes=[("u2T", j, t)])


        DGSZ = KW * 128
        if 4 * S >= 2 * DGSZ:
            dg_a1 = {2: v_off, 3: v_off + DGSZ}
        else:
            assert UREG - 4 * US >= DGSZ and 4 * S >= DGSZ
            dg_a1 = {2: v_off, 3: 4 * US}

        def DGA(j, k):
            if j < 2:
                return A3.v(j * DGSZ + k * 128, [(1, 128)])
            return A1.v(dg_a1[j] + k * 128, [(1, 128)])
        assert 2 * DGSZ <= 10112
        for j in range(4):
            for k in range(KW):
                add("dve", lambda e, j=j, k=k: e.tensor_scalar(out=DGA(j, k), in0=ident(), scalar1=cwT.v(j * 32 + k, [(1, 1)]),
                                                               scalar2=None, op0=ALU.mult),
                    reads=["cstb", "cwT"], writes=[("dg", j, k)])
        steps = [(t, j) for t in range(NT) for j in range(4)]

        def conv_step(i):
            t, j = steps[i]
            by = 2 + (i % 2)
            Py, oy = bank(by)
            for k in range(KW):
                rk = [("dg", j, k), ("uT", j, t), ("uT", -1)] + ([("uT", j, t - 1)] if t > 0 else [])
                mm(Py.v(oy, [(1, 512)]), DGA(j, k), A1.v(uT_off + j * US + t * 512 + k, [(1, 512)]), k == 0, k == KW - 1, rk, [("ps", by)])
            add("dve", lambda e: e.tensor_scalar(
                out=Yv(j, t), in0=Py.v(oy, [(1, 512)]),
                scalar1=cols.v(C_CB + j, [(1, 1)]), scalar2=None, op0=ALU.add),
                reads=[("ps", by), ("cols", C_CB)], writes=[ykey(j, t)])

        for i, (t, j) in enumerate(steps):
            conv_step(i)
            if j == 0 and t >= 1:
                ln_tile(t - 1)
        if not YRING:
            ln_tile(NT - 1)
        pre_a3 = load_w([wrows(win_h, 1024, 256)], KD, 256)
        if not YRING:
            Sd.barrier()
        Y_ALIAS = [("y", j, r) for j in range(4) for r in range(2)] if YRING else []
        DG_ALIAS = [("dg", j, k) for j in (2, 3) for k in range(KW)]
        UT_ALIAS = [("uT", j, t) for j in range(4) for t in range(NT)] + [("uT", -1)]

        def qk_body(which, g, dst_off):
            def body(s):
                for cc in range(2):
                    c = 2 * g + cc
                    for t in range(NT):
                        bq = (2 * c + t) % 4
                        Pq_, oq_ = bank(bq)
                        for k in range(KD):
                            mm(Pq_.v(oq_, [(1, 512)]), wbf[s].v(k * 256 + cc * 128, [(1, 128)]), A2.v(hT_off + k * S + t * 512, [(1, 512)]),
                               k == 0, k == KD - 1, [("wbf", s), ("hT", t)], [("ps", bq)])
                        if which == "q":
                            add("act", lambda e, P=Pq_, o=oq_, c=c, t=t: e.activation(
                                out=A1.v(dst_off + c * S + t * 512, [(1, 512)]), in_=P.v(o, [(1, 512)]), func=AF.Copy, scale=0.125),
                                reads=[("ps", bq)], writes=[("qT", c, t)])
                        else:
                            add("dve", lambda e, P=Pq_, o=oq_, c=c, t=t: e.tensor_copy(
                                out=A1.v(dst_off + c * S + t * 512, [(1, 512)]), in_=P.v(o, [(1, 512)])),
                                reads=[("ps", bq)], writes=[("kT", c)] + Y_ALIAS)
            return body

        def v_body(g):
            def body(s):
                for b in range(NB):
                    bv = 4 + (b % 4)
                    Pv, ov = bank(bv)
                    for k in range(KD):
                        mm(Pv.v(ov, [(1, 256)]), A2.v(hT_off + k * S + b * 128, [(1, 128)]), wbf[s].v(k * 256, [(1, 256)]),
                           k == 0, k == KD - 1, [("wbf", s), ("hT", b // 4)], [("ps", bv)])
                    add("dve", lambda e, Pv=Pv, ov=ov, b=b: e.tensor_copy(out=A1.v(v_off + b * 512 + 256 * g, [(1, 256)]), in_=Pv.v(ov, [(1, 256)])),
                        reads=[("ps", bv)], writes=[("v", b, g)] + DG_ALIAS)
            return body

        items = []
        for which, base_col, dst_off in (("q", 1024, qT_off), ("k", 1536, kT_off)):
            for g in range(2):
                items.append(((lambda base_col=base_col, g=g: load_w([wrows(win_h, base_col + 256 * g, 256)], KD, 256)), qk_body(which, g, dst_off)))
        for g in range(2):
            items.append(((lambda g=g: load_w([wrows(win_h, 2048 + 256 * g, 256)], KD, 256)), v_body(g)))
        run_stream(items, pre=pre_a3, after_first=((lambda: ln_tile(NT - 1)) if YRING else None))

        WCo, WAo = 0, 4096
        wcwa_jobs = [(h_, off_, g) for (h_, off_) in ((wc_h, WCo), (wa_h, WAo)) for g in range(4)]
        wcwa_slot = {}

        def wcwa_dma(n):
            h_, off_, g = wcwa_jobs[n]
            wcwa_slot[n] = load_w_dma([wrows(h_, 256 * g, 256)], 4, 256)

        def wcwa_fin(n):
            h_, off_, g = wcwa_jobs[n]
            sl = wcwa_slot[n]
            load_w_cast(sl, 4, 256)
            add("dve", lambda e: e.tensor_copy(out=A1.v(off_ + 256 * g, [(D, 4), (1, 256)]), in_=wbf[sl].v(0, [(256, 4), (1, 256)])),
                reads=[("wbf", sl)], writes=[("wcwa", off_)] + UT_ALIAS)

        Sd.barrier()
        Eo = [0, 2048]
        SPo = [4096, 5120, 6144]
        SCo = [7168, 8192]
        ATo = [9216, 10240]
        ZR = A3.v(11264, [(1, 64)])
        add("dve", lambda e: e.memset(ZR, 0.0), writes=["zr"])

        def v3(off, c0, n):
            return A3.v(off + c0, [(512, 2), (1, n)])

        def e3(i, c0, n):
            return A3F.v(Eo[i] // 2 + c0, [(512, 2), (1, n)])

        units = []
        for qt in range(NT):
            for c in range(4):
                seq = [(4 * qt + r, 128 * r, True) for r in (3, 2, 1, 0)] + [(kb, 0, False) for kb in range(4 * qt - 1, -1, -1)]
                for i, (kb, c0, dg) in enumerate(seq):
                    units.append(dict(qt=qt, c=c, kb=kb, c0=c0, diag=dg, first=(i == 0), last=(i == len(seq) - 1),
                                      pc0=(seq[i - 1][1] if i > 0 else None), g=qt * 4 + c))
        NU = len(units)
        zk = lambda u: [("ps", 2 * (u % 2)), ("ps", 2 * (u % 2) + 1)]
        ak = [("ps", 4), ("ps", 5)]
        Zv = lambda u, hh, c0, n: ps[u % 2].v(hh * 512 + c0, [(1, n)])
        Z2 = lambda u, c0, n: ps[u % 2].v(c0, [(512, 2), (1, n)])
        Av = lambda hh, c0, n: ps[2].v(hh * 512 + c0, [(1, n)])
        A2v = lambda c0, n: ps[2].v(c0, [(512, 2), (1, n)])

        def zmm(u, out_fn, keys, close):
            U = units[u]
            n = 512 - U["c0"]
            for hh in range(2):
                mm(out_fn(hh, U["c0"], n),
                   A1.v(kT_off + U["c"] * S + U["kb"] * 128, [(1, 128)], p0=hh * 64, npart=64),
                   A1.v(qT_off + U["c"] * S + U["qt"] * 512 + U["c0"], [(1, n)], p0=hh * 64, npart=64),
                   True, close and not U["diag"], [("kT", U["c"]), ("qT", U["c"], U["qt"])], keys)
                if U["diag"]:
                    mm(out_fn(hh, U["c0"], 128), ident(), maskneg(), False, close, ["cstb"], keys)

        def stA(u):
            zmm(u, lambda hh, c0, n: Zv(u, hh, c0, n), zk(u), True)

        def stB(u):
            U = units[u]
            c0 = U["c0"]
            n = 512 - c0
            add("act", lambda e: e.activation(out=e3(u % 2, c0, n), in_=Z2(u, c0, n), func=AF.Exp),
                reads=zk(u), writes=[("e", u % 2)])
            add("act", lambda e: e.activation(out=v3(SPo[u % 3], c0, n), in_=e3(u % 2, c0, n), func=AF.Ln, bias=1.0),
                reads=[("e", u % 2)], writes=[("sp", u % 3)])

        def stC(u):
            U = units[u]
            c0 = U["c0"]
            n = 512 - c0
            pc0 = U["pc0"]
            if not U["last"]:
                if U["first"]:
                    add("dve", lambda e: e.tensor_copy(out=v3(SCo[u % 2], c0, n), in_=v3(SPo[u % 3], c0, n)),
                        reads=[("sp", u % 3)], writes=[("sc", u % 2)])
                else:
                    if pc0 > c0:
                        add("dve", lambda e: e.tensor_copy(out=v3(SCo[u % 2], c0, pc0 - c0), in_=v3(SPo[u % 3], c0, pc0 - c0)),
                            reads=[("sp", u % 3)], writes=[("sc", u % 2)])
                    add("dve", lambda e: e.tensor_tensor(out=v3(SCo[u % 2], pc0, 512 - pc0), in0=v3(SPo[u % 3], pc0, 512 - pc0),
                                                         in1=v3(SCo[(u - 1) % 2], pc0, 512 - pc0), op=ALU.add),
                        reads=[("sp", u % 3), ("sc", (u - 1) % 2)], writes=[("sc", u % 2)])
            zmm(u, lambda hh, c0_, n_: Av(hh, c0_, n_), ak, False)
            for hh in range(2):
                mm(Av(hh, c0, n), trineg(), A3.v(SPo[u % 3] + hh * 512 + c0, [(1, n)]), False, U["first"],
                   ["cstb", ("sp", u % 3)], ak)
                if not U["first"]:
                    mm(Av(hh, pc0, 512 - pc0), onesneg(), A3.v(SCo[(u - 1) % 2] + hh * 512 + pc0, [(1, 512 - pc0)]), False, True,
                       ["cstb", ("sc", (u - 1) % 2)], ak)

        def stD(u):
            U = units[u]
            c0 = U["c0"]
            n = 512 - c0
            add("act", lambda e: e.activation(out=v3(ATo[u % 2], c0, n), in_=A2v(c0, n), func=AF.Exp),
                reads=ak, writes=[("at", u % 2)])

        def stE(u):
            U = units[u]
            c0 = U["c0"]
            n = 512 - c0
            g = U["g"]
            ob = 6 + (g % 2)
            Po, oo = bank(ob)
            if U["first"]:
                for hh in range(2):
                    mm(Po.v(oo, [(1, 512)], p0=hh * 64, npart=64), ZR, cstb.v(0, [(1, 512)]), True, False,
                       ["zr", "cstb"], [("ps", ob)])
            for hh in range(2):
                mm(Po.v(oo + c0, [(1, n)], p0=hh * 64, npart=64),
                   A1.v(v_off + U["kb"] * 512 + (2 * U["c"] + hh) * 64, [(1, 64)]),
                   A3.v(ATo[u % 2] + hh * 512 + c0, [(1, n)]), False, U["last"],
                   [("v", U["kb"], 0), ("v", U["kb"], 1), ("at", u % 2)], [("ps", ob)])
            if U["last"]:
                add("dve", lambda e: e.tensor_copy(out=A2.v(aT_off + U["c"] * S + U["qt"] * 512, [(1, 512)]), in_=Po.v(oo, [(1, 512)])),
                    reads=[("ps", ob)], writes=[("attT", U["c"], U["qt"])])

        stA(0)
        if NU > 1:
            stA(1)
        stB(0)
        stC(0)
        assert NU >= 4 * (len(wcwa_jobs) + 1)
        for u in range(NU):
            if u % 4 == 0:
                n = u // 4
                if n < len(wcwa_jobs):
                    wcwa_dma(n)
                if 1 <= n <= len(wcwa_jobs):
                    wcwa_fin(n - 1)
            if u >= 1:
                stE(u - 1)
            if u + 2 < NU:
                stA(u + 2)
            if u + 1 < NU:
                stB(u + 1)
            stD(u)
            if u + 1 < NU:
                stC(u + 1)
        stE(NU - 1)

        gate_loader = lambda c: load_w([wrows(win_h, 2560 + 128 * c, 128), wrows(win_h, 3584 + 128 * c, 128)], KD, 256)
        pre_b1 = gate_loader(0)
        G4T = A3F.v(9216, [(1, D)])
        add("sp", lambda e: e.dma_start(out=G4T, in_=bcast_row(g4_h)), writes=["g4t"], dma="g4t")
        add("sp", lambda e: e.dma_start(out=G13, in_=bcast_row(g3_h)), writes=["g13"], dma="g13")
        Sd.barrier()

        MT = lambda c, t: A3.v(c * S + t * 512, [(1, 512)])
        GS0 = (qT_off + 1) // 2 + 8
        SG1 = A1F.v(GS0, [(1, 512)])
        SG2 = A1F.v(GS0 + 512, [(1, 512)])
        M1 = A1F.v(GS0 + 1024, [(1, 512)])
        M2 = A1F.v(GS0 + 1536, [(1, 512)])
        items = []

        def b1_body(c):
            def body(s1):
                for t in range(NT):
                    b0 = 4 * (t % 2)
                    Pc, oc = bank(b0)
                    Pa_, oa_ = bank(b0 + 1)
                    Pg1, og1 = bank(b0 + 2)
                    Pg2, og2 = bank(b0 + 3)
                    for k in range(4):
                        mm(Pc.v(oc, [(1, 512)]), A1.v(WCo + k * D + 128 * c, [(1, 128)]), A2.v(u2_off + k * S + t * 512, [(1, 512)]),
                           k == 0, k == 3, [("wcwa", WCo), ("u2T", k, t)], [("ps", b0)])
                    for k in range(4):
                        mm(Pa_.v(oa_, [(1, 512)]), A1.v(WAo + k * D + 128 * c, [(1, 128)]), A2.v(aT_off + k * S + t * 512, [(1, 512)]),
                           k == 0, k == 3, [("wcwa", WAo), ("attT", k, t)], [("ps", b0 + 1)])
                    for k in range(KD):
                        mm(Pg1.v(og1, [(1, 512)]), wbf[s1].v(k * 256, [(1, 128)]), A2.v(hT_off + k * S + t * 512, [(1, 512)]),
                           k == 0, k == KD - 1, [("wbf", s1), ("hT", t)], [("ps", b0 + 2)])
                    for k in range(KD):
                        mm(Pg2.v(og2, [(1, 512)]), wbf[s1].v(k * 256 + 128, [(1, 128)]), A2.v(hT_off + k * S + t * 512, [(1, 512)]),
                           k == 0, k == KD - 1, [("wbf", s1), ("hT", t)], [("ps", b0 + 3)])
                    add("act", lambda e, P=Pg1, o=og1: e.activation(out=SG1, in_=P.v(o, [(1, 512)]), func=AF.Sigmoid),
                        reads=[("ps", b0 + 2)], writes=["sg1"])
                    add("act", lambda e, P=Pg2, o=og2: e.activation(out=SG2, in_=P.v(o, [(1, 512)]), func=AF.Sigmoid),
                        reads=[("ps", b0 + 3)], writes=["sg2"])
                    add("dve", lambda e, P=Pc, o=oc: e.scalar_tensor_tensor(
                        out=M1, in0=P.v(o, [(1, 512)]), scalar=cols.v(C_BC + c, [(1, 1)]), in1=SG1,
                        op0=ALU.add, op1=ALU.mult), reads=[("ps", b0), "sg1", ("cols", C_BC)], writes=["m1"])
                    add("dve", lambda e, P=Pa_, o=oa_: e.tensor_tensor(out=M2, in0=P.v(o, [(1, 512)]), in1=SG2, op=ALU.mult),
                        reads=[("ps", b0 + 1), "sg2"], writes=["m2"])
                    add("dve", lambda e, t=t: e.tensor_tensor(out=MT(c, t), in0=M1, in1=M2, op=ALU.add),
                        reads=["m1", "m2"], writes=[("mT", c, t)])
            return body
        for c in range(8):
            items.append(((lambda c=c: gate_loader(c)), b1_body(c)))
        wo_loader = lambda g: load_w([wrows(wo_h, 256 * g, 256)], KD, 256)
        run_stream(items, pre=pre_b1)
        pre_b2 = wo_loader(0)

        Sd.barrier()

        WO = lambda k, c0, n: A2.v(k * D + c0, [(1, n)])
        XR = [A2F.v(4096 + i * D, [(1, D)]) for i in range(2)]
        TMB = A2F.v(6144, [(1, D)])
        G2T = A2F.v(7168, [(1, D)])
        add("sp", lambda e: e.dma_start(out=G2T, in_=bcast_row(g2_h)), writes=["g2t"], dma="g2t")
        def wo_body(g):
            def body(sl):
                add("dve", lambda e: e.tensor_copy(out=A2.v(256 * g, [(D, KD), (1, 256)]), in_=wbf[sl].v(0, [(256, KD), (1, 256)])),
                    reads=[("wbf", sl)], writes=["wo"])
            return body
        run_stream([((lambda g=g: wo_loader(g)), wo_body(g)) for g in range(4)], pre=pre_b2)
        X1 = lambda b: A1F.v(b * D, [(1, D)])

        def post_norm_residual(pb0, res_fn, res_key, gt, gkey, tmp, out_ap, out_key, scol):
            P = ps[pb0 // 2]
            skey = ("stat", scol)
            sap = stat.v(scol, [(1, 1)])
            add("act", lambda e: e.activation(out=JUNK, in_=P.v(0, [(1, D)]), func=AF.Square, accum_out=sap),
                reads=[("ps", pb0), ("ps", pb0 + 1), skey], writes=["junk", skey])
            rstd_from(sap, 1.0 / D, skey)
            add("dve", lambda e: e.scalar_tensor_tensor(out=tmp, in0=P.v(0, [(1, D)]), scalar=sap, in1=gt, op0=ALU.mult, op1=ALU.mult),
                reads=[("ps", pb0), ("ps", pb0 + 1), skey, gkey], writes=["tmp"])
            add("dve", lambda e: e.tensor_tensor(out=out_ap, in0=tmp, in1=res_fn(), op=ALU.add),
                reads=["tmp", res_key], writes=[out_key])

        for b in range(NB):
            s = b % 2
            add("pool", lambda e, b=b, s=s: e.dma_start(out=XR[s], in_=x_h.ap()[b * 128:(b + 1) * 128, :]),
                writes=[("xr", s)], dma=("xr", s))
            pb0 = 4 + 2 * (b % 2)
            P = ps[pb0 // 2]
            t = b // 4
            for hcol in range(2):
                for k in range(KD):
                    mm(P.v(hcol * 512, [(1, 512)]), A3.v(k * S + b * 128, [(1, 128)]), WO(k, hcol * 512, 512),
                       k == 0, k == KD - 1, [("mT", k, t), "wo"], [("ps", pb0 + hcol)])
            post_norm_residual(pb0, lambda s=s: XR[s], ("xr", s), G2T, "g2t", TMB, X1(b), ("x1", b), NB + b)

        up_loader = lambda j: load_w([wrows(wu_h, 128 * j, 128), wrows(wu_h, DFF + 128 * j, 128)], KD, 256)
        pre_c = up_loader(0)
        Sd.barrier()

        HS = S // 2
        HB = NB // 2
        HT_ = HS // 512
        h2_off = 0
        ac_off = 8 * HS
        assert ac_off + KF * HS <= A2W
        TMC = A3F.v(0, [(1, D)])
        OT = [A3F.v(1024 + i * D, [(1, D)]) for i in range(2)]
        SL = A3F.v(3072, [(1, 512)])
        FF0 = lambda bb: A3F.v(3584 + bb * 512, [(1, 512)])
        outs = []
        for half in range(2):
            norm_transpose_seq([(None, (lambda b=half * HB + bb: X1(b)), ("x1", half * HB + bb), 2 * NB + half * HB + bb,
                                 A2.v(h2_off + bb * 128, [(HS, KD), (1, 128)]), ("h2T", bb // 4)) for bb in range(HB)], pbase=0)

            def up_body(j):
                def body(s):
                    for t in range(HT_):
                        p = (j * HT_ + t) % 3
                        bg_, bu_ = 2 * p, 2 * p + 1
                        Pg_, og_ = bank(bg_)
                        Pu_, ou_ = bank(bu_)
                        for k in range(KD):
                            mm(Pg_.v(og_, [(1, 512)]), wbf[s].v(k * 256, [(1, 128)]), A2.v(h2_off + k * HS + t * 512, [(1, 512)]),
                               k == 0, k == KD - 1, [("wbf", s), ("h2T", t)], [("ps", bg_)])
                        for k in range(KD):
                            mm(Pu_.v(ou_, [(1, 512)]), wbf[s].v(k * 256 + 128, [(1, 128)]), A2.v(h2_off + k * HS + t * 512, [(1, 512)]),
                               k == 0, k == KD - 1, [("wbf", s), ("h2T", t)], [("ps", bu_)])
                        add("act", lambda e, P=Pg_, o=og_: e.activation(out=SL, in_=P.v(o, [(1, 512)]), func=AF.Silu),
                            reads=[("ps", bg_)], writes=["sl"])
                        add("dve", lambda e, P=Pu_, o=ou_, t=t: e.tensor_tensor(out=A2.v(ac_off + j * HS + t * 512, [(1, 512)]), in0=P.v(o, [(1, 512)]),
                                                                                 in1=SL, op=ALU.mult),
                            reads=[("ps", bu_), "sl"], writes=[("actT", j, t)])
                return body

            def down_body(hc, j0, nj):
                def body(s):
                    for jj in range(nj):
                        j = j0 + jj
                        for bb in range(HB):
                            P, o = bank(bb)
                            mm(P.v(o, [(1, 512)]), A2.v(ac_off + j * HS + bb * 128, [(1, 128)]), wbf[s].v(jj * 512, [(1, 512)]),
                               j == 0, j == KF - 1, [("wbf", s), ("actT", j, bb // 4)], [("ps", bb)])
                    if j0 + nj == KF:
                        for bb in range(HB):
                            b = half * HB + bb
                            P, o = bank(bb)
                            sc = (3 + hc) * NB + b
                            add("act", lambda e, P=P, o=o, sc=sc: e.activation(out=A3.v(JUNK_O, [(1, 512)]), in_=P.v(o, [(1, 512)]), func=AF.Square,
                                                                                accum_out=stat.v(sc, [(1, 1)])),
                                reads=[("ps", bb), ("stat", sc)], writes=["junk", ("stat", sc)])
                            if hc == 0:
                                add("dve", lambda e, P=P, o=o, bb=bb: e.tensor_copy(out=FF0(bb), in_=P.v(o, [(1, 512)])),
                                    reads=[("ps", bb)], writes=[("ff0", bb)])
                        if hc == 1:
                            for bb in range(HB):
                                b = half * HB + bb
                                sa, sbk = ("stat", 3 * NB + b), ("stat", 4 * NB + b)
                                add("dve", lambda e, b=b: e.tensor_tensor(out=stat.v(4 * NB + b, [(1, 1)]), in0=stat.v(4 * NB + b, [(1, 1)]),
                                                                            in1=stat.v(3 * NB + b, [(1, 1)]), op=ALU.add),
                                    reads=[sa, sbk], writes=[sbk])
                            for bb in range(HB):
                                b = half * HB + bb
                                rstd_from(stat.v(4 * NB + b, [(1, 1)]), 1.0 / D, ("stat", 4 * NB + b))
                            for bb in range(HB):
                                b = half * HB + bb
                                P, o = bank(bb)
                                so = b % 2
                                sbk = ("stat", 4 * NB + b)
                                sap = stat.v(4 * NB + b, [(1, 1)])
                                for ch, src_fn, skeys in ((0, (lambda bb=bb: FF0(bb)), [("ff0", bb)]), (1, (lambda P=P, o=o: P.v(o, [(1, 512)])), [("ps", bb)])):
                                    add("dve", lambda e, ch=ch, src_fn=src_fn, sap=sap: e.scalar_tensor_tensor(
                                        out=A3F.v(ch * 512, [(1, 512)]), in0=src_fn(), scalar=sap, in1=A3F.v(9216 + ch * 512, [(1, 512)]),
                                        op0=ALU.mult, op1=ALU.mult), reads=skeys + [sbk, "g4t"], writes=[("tmc", ch)])
                                    add("dve", lambda e, ch=ch, b=b, so=so: e.tensor_tensor(
                                        out=A3F.v(1024 + so * D + ch * 512, [(1, 512)]), in0=A3F.v(ch * 512, [(1, 512)]),
                                        in1=A1F.v(b * D + ch * 512, [(1, 512)]), op=ALU.add),
                                        reads=[("tmc", ch), ("x1", b)], writes=[("ot", so)])
                                outs.append(add("pool", lambda e, b=b, so=so: e.dma_start(out=out_h.ap()[b * 128:(b + 1) * 128, :], in_=OT[so]),
                                                reads=[("ot", so)], dma=("out", so)))
                return body

            items = [((lambda j=j: up_loader(j)), up_body(j)) for j in range(KF)]
            for hc in range(2):
                for j0 in range(0, KF, 4):
                    nj = min(4, KF - j0)
                    items.append(((lambda hc=hc, j0=j0, nj=nj: load_w([wrows(wd_h, hc * 512, 512)[:, j0:j0 + nj, :]], nj, 512)),
                                  down_body(hc, j0, nj)))
            run_stream(items, pre=(pre_c if half == 0 else None))
        Sd.emit(nc, final_waits=outs)
    return nc


def make_consts():
    c = np.zeros((128, 640), np.float32)
    c[:, 0:128] = np.eye(128, dtype=np.float32)
    j = np.arange(128)[:, None]
    s = np.arange(128)[None, :]
    c[:, 128:256] = np.where(j >= s, -1.0, 0.0)
    c[:, 256:384] = -1.0
    c[:, 384:512] = np.where(j >= s, NEG, 0.0)
    c[:, 512:640] = 1.0 / CONV
    return c


_NC_CACHE = {}


def kernel(**inputs):
    x = np.ascontiguousarray(np.asarray(inputs["x"], dtype=np.float32))
    B, S, _ = x.shape
    if S not in _NC_CACHE:
        _NC_CACHE[S] = build(S)
    nc = _NC_CACHE[S]
    shared = {k: np.ascontiguousarray(np.asarray(v, dtype=np.float32)) for k, v in inputs.items() if k != "x"}
    shared["consts"] = make_consts()
    in_maps = []
    for b in range(B):
        m = dict(shared)
        m["x"] = x[b]
        in_maps.append(m)
    res = run_bass_kernel_spmd(nc, in_maps, core_ids=list(range(B)))
    return np.stack([np.asarray(r["out"]) for r in res.results], axis=0).astype(np.float32)
```

```python
import contextlib
import numpy as np
import concourse.bass as bass
import concourse.mybir as mybir
from concourse.bass_utils import run_bass_kernel_spmd

F32 = mybir.dt.float32
BF16 = mybir.dt.bfloat16
AF = mybir.ActivationFunctionType
ALU = mybir.AluOpType

D = 1024
KD = 8
CONV = 512
KW = 31
PAD = KW - 1
NH = 8
DH = 64
DFF = 2816
KF = 22
INC = 4608
EPS = 1e-6
NEG = -30000.0


class Op:
    __slots__ = ("eng", "fn", "deps", "signal", "sem", "val", "dma", "idx")

    def __init__(self, eng, fn, dma):
        self.eng = eng
        self.fn = fn
        self.dma = dma
        self.deps = set()
        self.signal = False
        self.sem = None
        self.val = 0


class Sched:
    ENG = ("pe", "act", "dve", "pool", "sp")

    def __init__(self):
        self.ops = {e: [] for e in self.ENG}
        self.last_w = {}
        self.readers = {}
        self.n = 0
        self.pending = {e: set() for e in self.ENG}

    def add(self, eng, fn, reads=(), writes=(), dma=None):
        op = Op(eng, fn, dma)
        op.idx = self.n
        self.n += 1
        deps = set(self.pending[eng])
        self.pending[eng] = set()
        for k in reads:
            w = self.last_w.get(k)
            if w is not None:
                deps.add(w)
            if isinstance(k, tuple) and k and k[0] == "ps":
                deps.update(r for r in self.readers.get(k, ()) if r.eng != eng)
        for k in writes:
            w = self.last_w.get(k)
            if w is not None:
                deps.add(w)
            deps.update(self.readers.get(k, ()))
        if eng == "pe":
            deps = {d for d in deps if not (d.eng == "pe" and d.dma is None)}
        newest = {}
        rest = set()
        for d in deps:
            if d.dma is None and d.eng in ("pe", "act", "dve"):
                if d.eng not in newest or newest[d.eng].idx < d.idx:
                    newest[d.eng] = d
            else:
                rest.add(d)
        deps = rest | set(newest.values())
        op.deps = deps
        for d in deps:
            d.signal = True
        for k in reads:
            self.readers.setdefault(k, []).append(op)
        for k in writes:
            self.last_w[k] = op
            self.readers[k] = []
        self.ops[eng].append(op)
        return op

    def barrier(self):
        lasts = set()
        for e in self.ENG:
            if self.ops[e]:
                lasts.add(self.ops[e][-1])
            for o in self.ops[e]:
                if o.dma is not None:
                    lasts.add(o)
        newest = {}
        keep = set()
        for o in lasts:
            if o.dma is None:
                keep.add(o)
            else:
                if o.dma not in newest or newest[o.dma].idx < o.idx:
                    newest[o.dma] = o
        keep.update(newest.values())
        for e in self.ENG:
            self.pending[e] = set(keep)

    def emit(self, nc, final_waits=()):
        with contextlib.ExitStack() as es:
            esem = {e: es.enter_context(nc.semaphore("s_" + e)) for e in self.ENG}
            dsem = {}
            dcnt = {}
            allops = []
            for e in self.ENG:
                allops.extend(self.ops[e])
            allops.sort(key=lambda o: o.idx)
            for op in allops:
                if op.dma is not None:
                    if op.dma not in dsem:
                        dsem[op.dma] = es.enter_context(nc.semaphore("d_%d" % len(dsem)))
                        dcnt[op.dma] = 0
                    dcnt[op.dma] += 16
                    op.sem = dsem[op.dma]
                    op.val = dcnt[op.dma]
            for e in self.ENG:
                c = 0
                for op in self.ops[e]:
                    if op.dma is None and op.signal:
                        c += 1
                        op.sem = esem[e]
                        op.val = c
            sched = self

            def run(eng_name, eng):
                waited = {}
                for op in sched.ops[eng_name]:
                    for d in sorted(op.deps, key=lambda o: o.idx):
                        sid = id(d.sem)
                        if waited.get(sid, 0) >= d.val:
                            continue
                        eng.wait_ge(d.sem, d.val)
                        waited[sid] = d.val
                    ins = op.fn(eng)
                    if op.dma is not None:
                        ins.then_inc(op.sem, 16)
                    elif op.signal:
                        ins.then_inc(op.sem, 1)
                if eng_name == "sp":
                    done = {}
                    for o in final_waits:
                        if done.get(id(o.sem), (None, 0))[1] < o.val:
                            done[id(o.sem)] = (o.sem, o.val)
                    for sem, val in done.values():
                        eng.wait_ge(sem, val)

            with nc.Block() as block:

                @block.tensor
                def _(e):
                    run("pe", e)

                @block.scalar
                def _(e):
                    run("act", e)

                @block.vector
                def _(e):
                    run("dve", e)

                @block.gpsimd
                def _(e):
                    run("pool", e)

                @block.sync
                def _(e):
                    run("sp", e)


class Buf:
    def __init__(self, t, w):
        self.t = t
        self.w = w

    def v(self, off, dims, p0=0, npart=128):
        return bass.AP(self.t, p0 * self.w + off, [[self.w, npart]] + [[s, c] for (s, c) in dims])


def build(S, dbg=False):
    NT = S // 512
    NB = S // 128
    nc = bass.Bass("TRN2", target_bir_lowering=False)
    dram = lambda name, shape: nc.dram_tensor(name, shape, F32, kind="ExternalInput")
    x_h = dram("x", [S, D])
    g1_h = dram("norm_mix_pre", [D])
    win_h = dram("w_in", [D, INC])
    cw_h = dram("conv_dw_w", [KW, CONV])
    cb_h = dram("conv_dw_b", [CONV])
    lg_h = dram("conv_ln_g", [CONV])
    lb_h = dram("conv_ln_b", [CONV])
    wc_h = dram("w_conv_branch", [CONV, D])
    bc_h = dram("b_conv_branch", [D])
    wa_h = dram("w_att_branch", [CONV, D])
    wo_h = dram("w_out", [D, D])
    g2_h = dram("norm_mix_post", [D])
    g3_h = dram("norm_ffn_pre", [D])
    wu_h = dram("w_ffn_up", [D, 2 * DFF])
    wd_h = dram("w_ffn_down", [DFF, D])
    g4_h = dram("norm_ffn_post", [D])
    cst_h = dram("consts", [128, 640])
    out_h = nc.dram_tensor("out", [S, D], F32, kind="ExternalOutput")

    es = contextlib.ExitStack()
    with es:
        def sb(name, w, dt):
            return Buf(es.enter_context(nc.sbuf_tensor(name, [128, w], dt)), w)

        US = PAD + S
        UREG = max(4 * US, 8192)
        A1W = UREG + 12 * S
        ar1 = es.enter_context(nc.sbuf_tensor("ar1", [128, A1W], BF16))
        A1 = Buf(ar1, A1W)
        A1F = Buf(ar1.bitcast(F32), A1W // 2)
        A2W = 16 * S
        ar2 = es.enter_context(nc.sbuf_tensor("ar2", [128, A2W], BF16))
        A2 = Buf(ar2, A2W)
        A2F = Buf(ar2.bitcast(F32), A2W // 2)
        A3W = 20480
        ar3 = es.enter_context(nc.sbuf_tensor("ar3", [128, A3W], BF16))
        A3 = Buf(ar3, A3W)
        A3F = Buf(ar3.bitcast(F32), A3W // 2)
        wst = [sb("wst%d" % i, 2048, F32) for i in range(2)]
        wbf = [sb("wbf%d" % i, 2048, BF16) for i in range(2)]
        cstf = sb("cstf", 640, F32)
        cstb = sb("cstb", 640, BF16)
        cols = sb("cols", 64, F32)
        stat = sb("stat", 5 * NB + 8, F32)
        cwb = sb("cwb", CONV, BF16)
        cwT = sb("cwT", 128, F32)
        g13 = sb("g13", D, F32)

        C_G1, C_G3, C_CB, C_LG, C_LB, C_BC = 0, 8, 16, 20, 24, 28
        uT_off, qT_off, kT_off, v_off = 0, UREG, UREG + 4 * S, UREG + 8 * S
        hT_off, aT_off, u2_off = 0, 8 * S, 12 * S
        IDN, TRI, ONE, MSK, ONM = 0, 128, 256, 384, 512

        XN_O, JUNK_O = 16384, 17408
        XN = A3.v(XN_O, [(1, D)])
        JUNK = A3.v(JUNK_O, [(1, D)])

        ps = [Buf(es.enter_context(nc.psum_tensor("ps%d" % i, [128, 1024], F32)), 1024) for i in range(4)]
        psb = [Buf(p.t.bitcast(BF16), 2048) for p in ps]

        def bank(i):
            return ps[i // 2], (i % 2) * 512

        Sd = Sched()
        add = Sd.add

        add("sp", lambda e: e.dma_start(out=cstf.v(0, [(1, 640)]), in_=cst_h.ap()), writes=["cstf"], dma="c0")
        add("dve", lambda e: e.tensor_copy(out=cstb.v(0, [(1, 640)]), in_=cstf.v(0, [(1, 640)])), reads=["cstf"], writes=["cstb"])
        add("dve", lambda e: e.memset(stat.v(0, [(1, 5 * NB + 8)]), 0.0), writes=[("stat", i) for i in range(5 * NB)])

        def colvec(h, n, slot):
            add("sp", lambda e: e.dma_start(out=cols.v(slot, [(1, n)]), in_=h.ap().rearrange("(j p) -> p j", p=128),
                                            allow_slow_non_contiguous=True),
                writes=[("cols", slot)], dma=("cols", slot))

        colvec(cb_h, 4, C_CB)
        colvec(lg_h, 4, C_LG)
        colvec(lb_h, 4, C_LB)
        colvec(bc_h, 8, C_BC)

        def bcast_row(h):
            return bass.AP(h, 0, [[0, 128], [1, D]])

        G13 = g13.v(0, [(1, D)])
        add("sp", lambda e: e.dma_start(out=G13, in_=bcast_row(g1_h)), writes=["g13"], dma="g13")

        ident = lambda: cstb.v(IDN, [(1, 128)])
        trineg = lambda: cstb.v(TRI, [(1, 128)])
        onesneg = lambda: cstb.v(ONE, [(1, 128)])
        maskneg = lambda: cstb.v(MSK, [(1, 128)])
        onesmean = lambda: cstb.v(ONM, [(1, 128)])

        wslot = [0]
        ncast = [0]

        def load_w(srcs, kch, ncols, gslot=None, cast_eng=None):
            s = wslot[0] % 2
            wslot[0] += 1
            c0 = 0
            pkeys = []
            for pi, src in enumerate(srcs):
                n_i = src.shape[-1]
                add("sp", (lambda s=s, c0=c0, n_i=n_i, src=src: lambda e: e.dma_start(
                    out=wst[s].v(c0, [(ncols, kch), (1, n_i)]), in_=src))(),
                    writes=[("wst", s, pi)], dma=("wst", s, pi))
                pkeys.append(("wst", s, pi))
                c0 += n_i
            allk = [("wst", s, 0), ("wst", s, 1)]
            if gslot is None:
                ce = cast_eng or ("act" if (ncast[0] % 2 == 0) else "dve")
                ncast[0] += 1
                if ce == "act":
                    add("act", lambda e, s=s: e.copy(out=wbf[s].v(0, [(1, kch * ncols)]), in_=wst[s].v(0, [(1, kch * ncols)])),
                        reads=allk, writes=[("wbf", s)])
                else:
                    add("dve", lambda e, s=s: e.tensor_copy(out=wbf[s].v(0, [(1, kch * ncols)]), in_=wst[s].v(0, [(1, kch * ncols)])),
                        reads=allk, writes=[("wbf", s)])
            else:
                for k in range(kch):
                    add("dve", lambda e, s=s, k=k: e.tensor_scalar(
                        out=wbf[s].v(k * ncols, [(1, ncols)]), in0=wst[s].v(k * ncols, [(1, ncols)]),
                        scalar1=cols.v(gslot + k, [(1, 1)]), scalar2=None, op0=ALU.mult),
                        reads=allk + [("cols", gslot)], writes=[("wbf", s)])
            return s

        def load_w_dma(srcs, kch, ncols):
            s_ = wslot[0] % 2
            wslot[0] += 1
            c0 = 0
            for pi, src in enumerate(srcs):
                n_i = src.shape[-1]
                add("sp", (lambda s_=s_, c0=c0, n_i=n_i, src=src: lambda e: e.dma_start(
                    out=wst[s_].v(c0, [(ncols, kch), (1, n_i)]), in_=src))(),
                    writes=[("wst", s_, pi)], dma=("wst", s_, pi))
                c0 += n_i
            return s_

        def load_w_cast(s_, kch, ncols):
            add("dve", lambda e: e.tensor_copy(out=wbf[s_].v(0, [(1, kch * ncols)]), in_=wst[s_].v(0, [(1, kch * ncols)])),
                reads=[("wst", s_, 0), ("wst", s_, 1)], writes=[("wbf", s_)])

        def run_stream(items, pre=None, after_first=None):
            nxt = items[0][0]() if pre is None else pre
            for i, (ld, body) in enumerate(items):
                cur = nxt
                if i + 1 < len(items):
                    nxt = items[i + 1][0]()
                body(cur)
                if i == 0 and after_first is not None:
                    after_first()

        def wrows(h, c0, n):
            return h.ap().rearrange("(k p) c -> p k c", p=128)[:, :, c0:c0 + n]

        def mm(out, lhsT, rhs, start, stop, reads, writes):
            add("pe", lambda e: e.matmul(out, lhsT=lhsT, rhs=rhs, start=start, stop=stop), reads=reads, writes=writes)

        def rstd_from(ap, scale, key):
            add("act", lambda e: e.activation(out=ap, in_=ap, func=AF.Ln, scale=scale, bias=EPS), reads=[key], writes=[key])
            add("act", lambda e: e.activation(out=ap, in_=ap, func=AF.Exp, scale=-0.5), reads=[key], writes=[key])

        XN2_O = 15360
        XNB = [(XN_O, "xn0"), (XN2_O, "xn1")]

        def norm_transpose_seq(blocks, pbase=6):
            def s1(i):
                pre, src_fn, src_key, scol, dst_ap, dst_key = blocks[i]
                if pre is not None:
                    pre()
                skey = ("stat", scol)
                sap = stat.v(scol, [(1, 1)])
                add("act", lambda e: e.activation(out=JUNK, in_=src_fn(), func=AF.Square, accum_out=sap),
                    reads=[src_key, skey], writes=["junk", skey])
                rstd_from(sap, 1.0 / D, skey)

            def s2(i):
                pre, src_fn, src_key, scol, dst_ap, dst_key = blocks[i]
                skey = ("stat", scol)
                sap = stat.v(scol, [(1, 1)])
                xo, xk = XNB[i % 2]
                pb = pbase + (i % 2)
                add("dve", lambda e: e.scalar_tensor_tensor(out=A3.v(xo, [(1, D)]), in0=src_fn(), scalar=sap, in1=G13, op0=ALU.mult, op1=ALU.mult),
                    reads=[src_key, skey, "g13"], writes=[xk])
                PB = psb[pb // 2]
                ob = (pb % 2) * 1024
                for k in range(KD):
                    add("pe", lambda e, k=k: e.transpose(out=PB.v(ob + k * 128, [(1, 128)]),
                                                         in_=A3.v(xo + k * 128, [(1, 128)]), identity=ident()),
                        reads=[xk, "cstb"], writes=[("ps", pb)])

            def s3(i):
                pre, src_fn, src_key, scol, dst_ap, dst_key = blocks[i]
                pb = pbase + (i % 2)
                PB = psb[pb // 2]
                ob = (pb % 2) * 1024
                add("dve", lambda e: e.tensor_copy(out=dst_ap, in_=PB.v(ob, [(128, KD), (1, 128)])),
                    reads=[("ps", pb)], writes=[dst_key])

            s1(0)
            for i in range(len(blocks)):
                if i + 1 < len(blocks):
                    s1(i + 1)
                s2(i)
                if i >= 1:
                    s3(i - 1)
            s3(len(blocks) - 1)

        XS = [A3F.v(0, [(1, D)]), A3F.v(D, [(1, D)]), A3F.v(3072, [(1, D)]), A3F.v(4096, [(1, D)])]

        def xload(b):
            s_ = b % 4
            return lambda: add("sp", lambda e: e.dma_start(out=XS[s_], in_=x_h.ap()[b * 128:(b + 1) * 128, :]),
                               writes=[("xs", s_)], dma=("xs", s_))
        norm_transpose_seq([(xload(b), (lambda b=b: XS[b % 4]), ("xs", b % 4), b,
                             A2.v(hT_off + b * 128, [(S, KD), (1, 128)]), ("hT", b // 4)) for b in range(NB)])

        add("dve", lambda e: e.memset(A1.v(uT_off, [(US, 4), (1, PAD)]), 0.0), writes=[("uT", -1)])
        SG = A3F.v(2048, [(1, 512)])
        def a1_body(j):
            def body(s):
                for t in range(NT):
                    ba, bg = 2 * (t % 2), 2 * (t % 2) + 1
                    Pa, oa = bank(ba)
                    Pg, og = bank(bg)
                    for k in range(KD):
                        mm(Pa.v(oa, [(1, 512)]), wbf[s].v(k * 256, [(1, 128)]), A2.v(hT_off + k * S + t * 512, [(1, 512)]),
                           k == 0, k == KD - 1, [("wbf", s), ("hT", t)], [("ps", ba)])
                    for k in range(KD):
                        mm(Pg.v(og, [(1, 512)]), wbf[s].v(k * 256 + 128, [(1, 128)]), A2.v(hT_off + k * S + t * 512, [(1, 512)]),
                           k == 0, k == KD - 1, [("wbf", s), ("hT", t)], [("ps", bg)])
                    add("act", lambda e, Pg=Pg, og=og: e.activation(out=SG, in_=Pg.v(og, [(1, 512)]), func=AF.Sigmoid),
                        reads=[("ps", bg)], writes=["sg"])
                    add("dve", lambda e, Pa=Pa, oa=oa, t=t: e.tensor_tensor(
                        out=A1.v(uT_off + j * US + PAD + t * 512, [(1, 512)]), in0=Pa.v(oa, [(1, 512)]), in1=SG, op=ALU.mult),
                        reads=[("ps", ba), "sg"], writes=[("uT", j, t)])
            return body
        run_stream([((lambda j=j: load_w([wrows(win_h, 128 * j, 128), wrows(win_h, 512 + 128 * j, 128)], KD, 256)), a1_body(j))
                    for j in range(4)])

        cwf = A3F.v(2560, [(1, CONV)], npart=KW)
        add("sp", lambda e: e.dma_start(out=cwf, in_=cw_h.ap()), writes=["cwf"], dma="cw")
        add("dve", lambda e: e.tensor_copy(out=cwb.v(0, [(1, CONV)], npart=KW), in_=cwf), reads=["cwf"], writes=["cwb"])
        for j in range(4):
            add("pe", lambda e, j=j: e.transpose(out=psb[3].v(j * 32, [(1, KW)]), in_=cwb.v(j * 128, [(1, 128)], npart=KW),
                                                 identity=cstb.v(IDN, [(1, KW)], npart=KW)),
                reads=["cwb", "cstb"], writes=[("ps", 6)])
        add("dve", lambda e: e.tensor_copy(out=cwT.v(0, [(32, 4), (1, KW)]), in_=psb[3].v(0, [(32, 4), (1, KW)])),
            reads=[("ps", 6)], writes=["cwT"])
        Y_off = qT_off // 2
        YRING = (NT > 2)
        assert (not YRING) or 8 * 512 * 2 <= 4 * S

        def Yv(j, t):
            if YRING:
                return A1F.v(kT_off // 2 + ((t % 2) * 4 + j) * 512, [(1, 512)])
            return A1F.v(Y_off + j * S + t * 512, [(1, 512)])
        ykey = lambda j, t: ("y", j, (t % 2) if YRING else t)
        DG = lambda k: A3.v(6144 + k * 128, [(1, 128)])
        YB = lambda j: A3.v(10112 + j * 512, [(1, 512)])
        YQ = lambda j: A3.v(12160 + j * 512, [(1, 512)])
        MU = A3F.v(7104, [(1, 512)])
        RS = A3F.v(7616, [(1, 512)])
        T1 = A3F.v(9216, [(1, 512)])
        def ln_tile(t):
            for j in range(4):
                add("dve", lambda e, j=j, t=t: e.tensor_copy(out=YB(j), in_=Yv(j, t)),
                    reads=[ykey(j, t)], writes=[("yb", j)])
                add("act", lambda e, j=j, t=t: e.activation(out=YQ(j), in_=Yv(j, t), func=AF.Square),
                    reads=[ykey(j, t)], writes=[("yq", j)])
            Pm, om = bank(4)
            Pq, oq = bank(5)
            for j in range(4):
                mm(Pm.v(om, [(1, 512)]), onesmean(), YB(j), j == 0, j == 3, ["cstb", ("yb", j)], [("ps", 4)])
            for j in range(4):
                mm(Pq.v(oq, [(1, 512)]), onesmean(), YQ(j), j == 0, j == 3, ["cstb", ("yq", j)], [("ps", 5)])
            add("act", lambda e, Pm=Pm, om=om: e.copy(out=MU, in_=Pm.v(om, [(1, 512)])), reads=[("ps", 4)], writes=["mu"])
            add("dve", lambda e: e.tensor_tensor(out=T1, in0=MU, in1=MU, op=ALU.mult), reads=["mu"], writes=["t1"])
            add("dve", lambda e, Pq=Pq, oq=oq: e.tensor_tensor(out=RS, in0=Pq.v(oq, [(1, 512)]), in1=T1, op=ALU.subtract),
                reads=[("ps", 5), "t1"], writes=["rs"])
            rstd_from(RS, 1.0, "rs")
            for j in range(4):
                add("dve", lambda e, j=j, t=t: e.tensor_tensor(out=T1, in0=Yv(j, t), in1=MU, op=ALU.subtract),
                    reads=[ykey(j, t), "mu"], writes=["t1"])
                add("dve", lambda e: e.tensor_tensor(out=T1, in0=T1, in1=RS, op=ALU.mult), reads=["t1", "rs"], writes=["t1"])
                add("act", lambda e, j=j, t=t: e.activation(out=A2.v(u2_off + j * S + t * 512, [(1, 512)]), in_=T1, func=AF.Silu,
                                                            scale=cols.v(C_LG + j, [(1, 1)]), bias=cols.v(C_LB + j, [(1, 1)])),
                    reads=["t1", ("cols", C_LG), ("cols", C_LB)], writes=[("u2T", j, t)])


        DGSZ = KW * 128
        if 4 * S >= 2 * DGSZ:
            dg_a1 = {2: v_off, 3: v_off + DGSZ}
        else:
            assert UREG - 4 * US >= DGSZ and 4 * S >= DGSZ
            dg_a1 = {2: v_off, 3: 4 * US}

        def DGA(j, k):
            if j < 2:
                return A3.v(j * DGSZ + k * 128, [(1, 128)])
            return A1.v(dg_a1[j] + k * 128, [(1, 128)])
        assert 2 * DGSZ <= 10112
        for j in range(4):
            for k in range(KW):
                add("dve", lambda e, j=j, k=k: e.tensor_scalar(out=DGA(j, k), in0=ident(), scalar1=cwT.v(j * 32 + k, [(1, 1)]),
                                                               scalar2=None, op0=ALU.mult),
                    reads=["cstb", "cwT"], writes=[("dg", j, k)])
        steps = [(t, j) for t in range(NT) for j in range(4)]

        def conv_step(i):
            t, j = steps[i]
            by = 2 + (i % 2)
            Py, oy = bank(by)
            for k in range(KW):
                rk = [("dg", j, k), ("uT", j, t), ("uT", -1)] + ([("uT", j, t - 1)] if t > 0 else [])
                mm(Py.v(oy, [(1, 512)]), DGA(j, k), A1.v(uT_off + j * US + t * 512 + k, [(1, 512)]), k == 0, k == KW - 1, rk, [("ps", by)])
            add("dve", lambda e: e.tensor_scalar(
                out=Yv(j, t), in0=Py.v(oy, [(1, 512)]),
                scalar1=cols.v(C_CB + j, [(1, 1)]), scalar2=None, op0=ALU.add),
                reads=[("ps", by), ("cols", C_CB)], writes=[ykey(j, t)])

        for i, (t, j) in enumerate(steps):
            conv_step(i)
            if j == 0 and t >= 1:
                ln_tile(t - 1)
        if not YRING:
            ln_tile(NT - 1)
        pre_a3 = load_w([wrows(win_h, 1024, 256)], KD, 256)
        if not YRING:
            Sd.barrier()
        Y_ALIAS = [("y", j, r) for j in range(4) for r in range(2)] if YRING else []
        DG_ALIAS = [("dg", j, k) for j in (2, 3) for k in range(KW)]
        UT_ALIAS = [("uT", j, t) for j in range(4) for t in range(NT)] + [("uT", -1)]

        def qk_body(which, g, dst_off):
            def body(s):
                for cc in range(2):
                    c = 2 * g + cc
                    for t in range(NT):
                        bq = (2 * c + t) % 4
                        Pq_, oq_ = bank(bq)
                        for k in range(KD):
                            mm(Pq_.v(oq_, [(1, 512)]), wbf[s].v(k * 256 + cc * 128, [(1, 128)]), A2.v(hT_off + k * S + t * 512, [(1, 512)]),
                               k == 0, k == KD - 1, [("wbf", s), ("hT", t)], [("ps", bq)])
                        if which == "q":
                            add("act", lambda e, P=Pq_, o=oq_, c=c, t=t: e.activation(
                                out=A1.v(dst_off + c * S + t * 512, [(1, 512)]), in_=P.v(o, [(1, 512)]), func=AF.Copy, scale=0.125),
                                reads=[("ps", bq)], writes=[("qT", c, t)])
                        else:
                            add("dve", lambda e, P=Pq_, o=oq_, c=c, t=t: e.tensor_copy(
                                out=A1.v(dst_off + c * S + t * 512, [(1, 512)]), in_=P.v(o, [(1, 512)])),
                                reads=[("ps", bq)], writes=[("kT", c)] + Y_ALIAS)
            return body

        def v_body(g):
            def body(s):
                for b in range(NB):
                    bv = 4 + (b % 4)
                    Pv, ov = bank(bv)
                    for k in range(KD):
                        mm(Pv.v(ov, [(1, 256)]), A2.v(hT_off + k * S + b * 128, [(1, 128)]), wbf[s].v(k * 256, [(1, 256)]),
                           k == 0, k == KD - 1, [("wbf", s), ("hT", b // 4)], [("ps", bv)])
                    add("dve", lambda e, Pv=Pv, ov=ov, b=b: e.tensor_copy(out=A1.v(v_off + b * 512 + 256 * g, [(1, 256)]), in_=Pv.v(ov, [(1, 256)])),
                        reads=[("ps", bv)], writes=[("v", b, g)] + DG_ALIAS)
            return body

        items = []
        for which, base_col, dst_off in (("q", 1024, qT_off), ("k", 1536, kT_off)):
            for g in range(2):
                items.append(((lambda base_col=base_col, g=g: load_w([wrows(win_h, base_col + 256 * g, 256)], KD, 256)), qk_body(which, g, dst_off)))
        for g in range(2):
            items.append(((lambda g=g: load_w([wrows(win_h, 2048 + 256 * g, 256)], KD, 256)), v_body(g)))
        run_stream(items, pre=pre_a3, after_first=((lambda: ln_tile(NT - 1)) if YRING else None))

        WCo, WAo = 0, 4096
        wcwa_jobs = [(h_, off_, g) for (h_, off_) in ((wc_h, WCo), (wa_h, WAo)) for g in range(4)]
        wcwa_slot = {}

        def wcwa_dma(n):
            h_, off_, g = wcwa_jobs[n]
            wcwa_slot[n] = load_w_dma([wrows(h_, 256 * g, 256)], 4, 256)

        def wcwa_fin(n):
            h_, off_, g = wcwa_jobs[n]
            sl = wcwa_slot[n]
            load_w_cast(sl, 4, 256)
            add("dve", lambda e: e.tensor_copy(out=A1.v(off_ + 256 * g, [(D, 4), (1, 256)]), in_=wbf[sl].v(0, [(256, 4), (1, 256)])),
                reads=[("wbf", sl)], writes=[("wcwa", off_)] + UT_ALIAS)

        Sd.barrier()
        Eo = [0, 2048]
        SPo = [4096, 5120, 6144]
        SCo = [7168, 8192]
        ATo = [9216, 10240]
        ZR = A3.v(11264, [(1, 64)])
        add("dve", lambda e: e.memset(ZR, 0.0), writes=["zr"])

        def v3(off, c0, n):
            return A3.v(off + c0, [(512, 2), (1, n)])

        def e3(i, c0, n):
            return A3F.v(Eo[i] // 2 + c0, [(512, 2), (1, n)])

        units = []
        for qt in range(NT):
            for c in range(4):
                seq = [(4 * qt + r, 128 * r, True) for r in (3, 2, 1, 0)] + [(kb, 0, False) for kb in range(4 * qt - 1, -1, -1)]
                for i, (kb, c0, dg) in enumerate(seq):
                    units.append(dict(qt=qt, c=c, kb=kb, c0=c0, diag=dg, first=(i == 0), last=(i == len(seq) - 1),
                                      pc0=(seq[i - 1][1] if i > 0 else None), g=qt * 4 + c))
        NU = len(units)
        zk = lambda u: [("ps", 2 * (u % 2)), ("ps", 2 * (u % 2) + 1)]
        ak = [("ps", 4), ("ps", 5)]
        Zv = lambda u, hh, c0, n: ps[u % 2].v(hh * 512 + c0, [(1, n)])
        Z2 = lambda u, c0, n: ps[u % 2].v(c0, [(512, 2), (1, n)])
        Av = lambda hh, c0, n: ps[2].v(hh * 512 + c0, [(1, n)])
        A2v = lambda c0, n: ps[2].v(c0, [(512, 2), (1, n)])

        def zmm(u, out_fn, keys, close):
            U = units[u]
            n = 512 - U["c0"]
            for hh in range(2):
                mm(out_fn(hh, U["c0"], n),
                   A1.v(kT_off + U["c"] * S + U["kb"] * 128, [(1, 128)], p0=hh * 64, npart=64),
                   A1.v(qT_off + U["c"] * S + U["qt"] * 512 + U["c0"], [(1, n)], p0=hh * 64, npart=64),
                   True, close and not U["diag"], [("kT", U["c"]), ("qT", U["c"], U["qt"])], keys)
                if U["diag"]:
                    mm(out_fn(hh, U["c0"], 128), ident(), maskneg(), False, close, ["cstb"], keys)

        def stA(u):
            zmm(u, lambda hh, c0, n: Zv(u, hh, c0, n), zk(u), True)

        def stB(u):
            U = units[u]
            c0 = U["c0"]
            n = 512 - c0
            add("act", lambda e: e.activation(out=e3(u % 2, c0, n), in_=Z2(u, c0, n), func=AF.Exp),
                reads=zk(u), writes=[("e", u % 2)])
            add("act", lambda e: e.activation(out=v3(SPo[u % 3], c0, n), in_=e3(u % 2, c0, n), func=AF.Ln, bias=1.0),
                reads=[("e", u % 2)], writes=[("sp", u % 3)])

        def stC(u):
            U = units[u]
            c0 = U["c0"]
            n = 512 - c0
            pc0 = U["pc0"]
            if not U["last"]:
                if U["first"]:
                    add("dve", lambda e: e.tensor_copy(out=v3(SCo[u % 2], c0, n), in_=v3(SPo[u % 3], c0, n)),
                        reads=[("sp", u % 3)], writes=[("sc", u % 2)])
                else:
                    if pc0 > c0:
                        add("dve", lambda e: e.tensor_copy(out=v3(SCo[u % 2], c0, pc0 - c0), in_=v3(SPo[u % 3], c0, pc0 - c0)),
                            reads=[("sp", u % 3)], writes=[("sc", u % 2)])
                    add("dve", lambda e: e.tensor_tensor(out=v3(SCo[u % 2], pc0, 512 - pc0), in0=v3(SPo[u % 3], pc0, 512 - pc0),
                                                         in1=v3(SCo[(u - 1) % 2], pc0, 512 - pc0), op=ALU.add),
                        reads=[("sp", u % 3), ("sc", (u - 1) % 2)], writes=[("sc", u % 2)])
            zmm(u, lambda hh, c0_, n_: Av(hh, c0_, n_), ak, False)
            for hh in range(2):
                mm(Av(hh, c0, n), trineg(), A3.v(SPo[u % 3] + hh * 512 + c0, [(1, n)]), False, U["first"],
                   ["cstb", ("sp", u % 3)], ak)
                if not U["first"]:
                    mm(Av(hh, pc0, 512 - pc0), onesneg(), A3.v(SCo[(u - 1) % 2] + hh * 512 + pc0, [(1, 512 - pc0)]), False, True,
                       ["cstb", ("sc", (u - 1) % 2)], ak)

        def stD(u):
            U = units[u]
            c0 = U["c0"]
            n = 512 - c0
            add("act", lambda e: e.activation(out=v3(ATo[u % 2], c0, n), in_=A2v(c0, n), func=AF.Exp),
                reads=ak, writes=[("at", u % 2)])

        def stE(u):
            U = units[u]
            c0 = U["c0"]
            n = 512 - c0
            g = U["g"]
            ob = 6 + (g % 2)
            Po, oo = bank(ob)
            if U["first"]:
                for hh in range(2):
                    mm(Po.v(oo, [(1, 512)], p0=hh * 64, npart=64), ZR, cstb.v(0, [(1, 512)]), True, False,
                       ["zr", "cstb"], [("ps", ob)])
            for hh in range(2):
                mm(Po.v(oo + c0, [(1, n)], p0=hh * 64, npart=64),
                   A1.v(v_off + U["kb"] * 512 + (2 * U["c"] + hh) * 64, [(1, 64)]),
                   A3.v(ATo[u % 2] + hh * 512 + c0, [(1, n)]), False, U["last"],
                   [("v", U["kb"], 0), ("v", U["kb"], 1), ("at", u % 2)], [("ps", ob)])
            if U["last"]:
                add("dve", lambda e: e.tensor_copy(out=A2.v(aT_off + U["c"] * S + U["qt"] * 512, [(1, 512)]), in_=Po.v(oo, [(1, 512)])),
                    reads=[("ps", ob)], writes=[("attT", U["c"], U["qt"])])

        stA(0)
        if NU > 1:
            stA(1)
        stB(0)
        stC(0)
        assert NU >= 4 * (len(wcwa_jobs) + 1)
        for u in range(NU):
            if u % 4 == 0:
                n = u // 4
                if n < len(wcwa_jobs):
                    wcwa_dma(n)
                if 1 <= n <= len(wcwa_jobs):
                    wcwa_fin(n - 1)
            if u >= 1:
                stE(u - 1)
            if u + 2 < NU:
                stA(u + 2)
            if u + 1 < NU:
                stB(u + 1)
            stD(u)
            if u + 1 < NU:
                stC(u + 1)
        stE(NU - 1)

        gate_loader = lambda c: load_w([wrows(win_h, 2560 + 128 * c, 128), wrows(win_h, 3584 + 128 * c, 128)], KD, 256)
        pre_b1 = gate_loader(0)
        G4T = A3F.v(9216, [(1, D)])
        add("sp", lambda e: e.dma_start(out=G4T, in_=bcast_row(g4_h)), writes=["g4t"], dma="g4t")
        add("sp", lambda e: e.dma_start(out=G13, in_=bcast_row(g3_h)), writes=["g13"], dma="g13")
        Sd.barrier()

        MT = lambda c, t: A3.v(c * S + t * 512, [(1, 512)])
        GS0 = (qT_off + 1) // 2 + 8
        SG1 = A1F.v(GS0, [(1, 512)])
        SG2 = A1F.v(GS0 + 512, [(1, 512)])
        M1 = A1F.v(GS0 + 1024, [(1, 512)])
        M2 = A1F.v(GS0 + 1536, [(1, 512)])
        items = []

        def b1_body(c):
            def body(s1):
                for t in range(NT):
                    b0 = 4 * (t % 2)
                    Pc, oc = bank(b0)
                    Pa_, oa_ = bank(b0 + 1)
                    Pg1, og1 = bank(b0 + 2)
                    Pg2, og2 = bank(b0 + 3)
                    for k in range(4):
                        mm(Pc.v(oc, [(1, 512)]), A1.v(WCo + k * D + 128 * c, [(1, 128)]), A2.v(u2_off + k * S + t * 512, [(1, 512)]),
                           k == 0, k == 3, [("wcwa", WCo), ("u2T", k, t)], [("ps", b0)])
                    for k in range(4):
                        mm(Pa_.v(oa_, [(1, 512)]), A1.v(WAo + k * D + 128 * c, [(1, 128)]), A2.v(aT_off + k * S + t * 512, [(1, 512)]),
                           k == 0, k == 3, [("wcwa", WAo), ("attT", k, t)], [("ps", b0 + 1)])
                    for k in range(KD):
                        mm(Pg1.v(og1, [(1, 512)]), wbf[s1].v(k * 256, [(1, 128)]), A2.v(hT_off + k * S + t * 512, [(1, 512)]),
                           k == 0, k == KD - 1, [("wbf", s1), ("hT", t)], [("ps", b0 + 2)])
                    for k in range(KD):
                        mm(Pg2.v(og2, [(1, 512)]), wbf[s1].v(k * 256 + 128, [(1, 128)]), A2.v(hT_off + k * S + t * 512, [(1, 512)]),
                           k == 0, k == KD - 1, [("wbf", s1), ("hT", t)], [("ps", b0 + 3)])
                    add("act", lambda e, P=Pg1, o=og1: e.activation(out=SG1, in_=P.v(o, [(1, 512)]), func=AF.Sigmoid),
                        reads=[("ps", b0 + 2)], writes=["sg1"])
                    add("act", lambda e, P=Pg2, o=og2: e.activation(out=SG2, in_=P.v(o, [(1, 512)]), func=AF.Sigmoid),
                        reads=[("ps", b0 + 3)], writes=["sg2"])
                    add("dve", lambda e, P=Pc, o=oc: e.scalar_tensor_tensor(
                        out=M1, in0=P.v(o, [(1, 512)]), scalar=cols.v(C_BC + c, [(1, 1)]), in1=SG1,
                        op0=ALU.add, op1=ALU.mult), reads=[("ps", b0), "sg1", ("cols", C_BC)], writes=["m1"])
                    add("dve", lambda e, P=Pa_, o=oa_: e.tensor_tensor(out=M2, in0=P.v(o, [(1, 512)]), in1=SG2, op=ALU.mult),
                        reads=[("ps", b0 + 1), "sg2"], writes=["m2"])
                    add("dve", lambda e, t=t: e.tensor_tensor(out=MT(c, t), in0=M1, in1=M2, op=ALU.add),
                        reads=["m1", "m2"], writes=[("mT", c, t)])
            return body
        for c in range(8):
            items.append(((lambda c=c: gate_loader(c)), b1_body(c)))
        wo_loader = lambda g: load_w([wrows(wo_h, 256 * g, 256)], KD, 256)
        run_stream(items, pre=pre_b1)
        pre_b2 = wo_loader(0)

        Sd.barrier()

        WO = lambda k, c0, n: A2.v(k * D + c0, [(1, n)])
        XR = [A2F.v(4096 + i * D, [(1, D)]) for i in range(2)]
        TMB = A2F.v(6144, [(1, D)])
        G2T = A2F.v(7168, [(1, D)])
        add("sp", lambda e: e.dma_start(out=G2T, in_=bcast_row(g2_h)), writes=["g2t"], dma="g2t")
        def wo_body(g):
            def body(sl):
                add("dve", lambda e: e.tensor_copy(out=A2.v(256 * g, [(D, KD), (1, 256)]), in_=wbf[sl].v(0, [(256, KD), (1, 256)])),
                    reads=[("wbf", sl)], writes=["wo"])
            return body
        run_stream([((lambda g=g: wo_loader(g)), wo_body(g)) for g in range(4)], pre=pre_b2)
        X1 = lambda b: A1F.v(b * D, [(1, D)])

        def post_norm_residual(pb0, res_fn, res_key, gt, gkey, tmp, out_ap, out_key, scol):
            P = ps[pb0 // 2]
            skey = ("stat", scol)
            sap = stat.v(scol, [(1, 1)])
            add("act", lambda e: e.activation(out=JUNK, in_=P.v(0, [(1, D)]), func=AF.Square, accum_out=sap),
                reads=[("ps", pb0), ("ps", pb0 + 1), skey], writes=["junk", skey])
            rstd_from(sap, 1.0 / D, skey)
            add("dve", lambda e: e.scalar_tensor_tensor(out=tmp, in0=P.v(0, [(1, D)]), scalar=sap, in1=gt, op0=ALU.mult, op1=ALU.mult),
                reads=[("ps", pb0), ("ps", pb0 + 1), skey, gkey], writes=["tmp"])
            add("dve", lambda e: e.tensor_tensor(out=out_ap, in0=tmp, in1=res_fn(), op=ALU.add),
                reads=["tmp", res_key], writes=[out_key])

        for b in range(NB):
            s = b % 2
            add("sp", lambda e, b=b, s=s: e.dma_start(out=XR[s], in_=x_h.ap()[b * 128:(b + 1) * 128, :]),
                writes=[("xr", s)], dma=("xr", s))
            pb0 = 4 + 2 * (b % 2)
            P = ps[pb0 // 2]
            t = b // 4
            for hcol in range(2):
                for k in range(KD):
                    mm(P.v(hcol * 512, [(1, 512)]), A3.v(k * S + b * 128, [(1, 128)]), WO(k, hcol * 512, 512),
                       k == 0, k == KD - 1, [("mT", k, t), "wo"], [("ps", pb0 + hcol)])
            post_norm_residual(pb0, lambda s=s: XR[s], ("xr", s), G2T, "g2t", TMB, X1(b), ("x1", b), NB + b)

        up_loader = lambda j: load_w([wrows(wu_h, 128 * j, 128), wrows(wu_h, DFF + 128 * j, 128)], KD, 256)
        pre_c = up_loader(0)
        Sd.barrier()

        HS = S // 2
        HB = NB // 2
        HT_ = HS // 512
        h2_off = 0
        ac_off = 8 * HS
        assert ac_off + KF * HS <= A2W
        TMC = A3F.v(0, [(1, D)])
        OT = [A3F.v(1024 + i * D, [(1, D)]) for i in range(2)]
        SL = A3F.v(3072, [(1, 512)])
        FF0 = lambda bb: A3F.v(3584 + bb * 512, [(1, 512)])
        outs = []
        for half in range(2):
            norm_transpose_seq([(None, (lambda b=half * HB + bb: X1(b)), ("x1", half * HB + bb), 2 * NB + half * HB + bb,
                                 A2.v(h2_off + bb * 128, [(HS, KD), (1, 128)]), ("h2T", bb // 4)) for bb in range(HB)], pbase=0)

            def up_body(j):
                def body(s):
                    for t in range(HT_):
                        p = (j * HT_ + t) % 3
                        bg_, bu_ = 2 * p, 2 * p + 1
                        Pg_, og_ = bank(bg_)
                        Pu_, ou_ = bank(bu_)
                        for k in range(KD):
                            mm(Pg_.v(og_, [(1, 512)]), wbf[s].v(k * 256, [(1, 128)]), A2.v(h2_off + k * HS + t * 512, [(1, 512)]),
                               k == 0, k == KD - 1, [("wbf", s), ("h2T", t)], [("ps", bg_)])
                        for k in range(KD):
                            mm(Pu_.v(ou_, [(1, 512)]), wbf[s].v(k * 256 + 128, [(1, 128)]), A2.v(h2_off + k * HS + t * 512, [(1, 512)]),
                               k == 0, k == KD - 1, [("wbf", s), ("h2T", t)], [("ps", bu_)])
                        add("act", lambda e, P=Pg_, o=og_: e.activation(out=SL, in_=P.v(o, [(1, 512)]), func=AF.Silu),
                            reads=[("ps", bg_)], writes=["sl"])
                        add("dve", lambda e, P=Pu_, o=ou_, t=t: e.tensor_tensor(out=A2.v(ac_off + j * HS + t * 512, [(1, 512)]), in0=P.v(o, [(1, 512)]),
                                                                                 in1=SL, op=ALU.mult),
                            reads=[("ps", bu_), "sl"], writes=[("actT", j, t)])
                return body

            def down_body(hc, j0, nj):
                def body(s):
                    for jj in range(nj):
                        j = j0 + jj
                        for bb in range(HB):
                            P, o = bank(bb)
                            mm(P.v(o, [(1, 512)]), A2.v(ac_off + j * HS + bb * 128, [(1, 128)]), wbf[s].v(jj * 512, [(1, 512)]),
                               j == 0, j == KF - 1, [("wbf", s), ("actT", j, bb // 4)], [("ps", bb)])
                    if j0 + nj == KF:
                        for bb in range(HB):
                            b = half * HB + bb
                            P, o = bank(bb)
                            sc = (3 + hc) * NB + b
                            add("act", lambda e, P=P, o=o, sc=sc: e.activation(out=A3.v(JUNK_O, [(1, 512)]), in_=P.v(o, [(1, 512)]), func=AF.Square,
                                                                                accum_out=stat.v(sc, [(1, 1)])),
                                reads=[("ps", bb), ("stat", sc)], writes=["junk", ("stat", sc)])
                            if hc == 0:
                                add("dve", lambda e, P=P, o=o, bb=bb: e.tensor_tensor(out=FF0(bb), in0=P.v(o, [(1, 512)]),
                                                                                     in1=A3F.v(9216, [(1, 512)]), op=ALU.mult),
                                    reads=[("ps", bb), "g4t"], writes=[("ff0", bb)])
                        if hc == 1:
                            for bb in range(HB):
                                b = half * HB + bb
                                sa, sbk = ("stat", 3 * NB + b), ("stat", 4 * NB + b)
                                add("dve", lambda e, b=b: e.tensor_tensor(out=stat.v(4 * NB + b, [(1, 1)]), in0=stat.v(4 * NB + b, [(1, 1)]),
                                                                            in1=stat.v(3 * NB + b, [(1, 1)]), op=ALU.add),
                                    reads=[sa, sbk], writes=[sbk])
                            for bb in range(HB):
                                b = half * HB + bb
                                rstd_from(stat.v(4 * NB + b, [(1, 1)]), 1.0 / D, ("stat", 4 * NB + b))
                            for bb in range(HB):
                                b = half * HB + bb
                                P, o = bank(bb)
                                so = b % 2
                                sbk = ("stat", 4 * NB + b)
                                sap = stat.v(4 * NB + b, [(1, 1)])
                                add("dve", lambda e, bb=bb, b=b, so=so, sap=sap: e.scalar_tensor_tensor(
                                    out=A3F.v(1024 + so * D, [(1, 512)]), in0=FF0(bb), scalar=sap, in1=A1F.v(b * D, [(1, 512)]),
                                    op0=ALU.mult, op1=ALU.add), reads=[("ff0", bb), sbk, ("x1", b)], writes=[("ot", so)])
                                for ch, src_fn, skeys in ((1, (lambda P=P, o=o: P.v(o, [(1, 512)])), [("ps", bb)]),):
                                    add("dve", lambda e, ch=ch, src_fn=src_fn, sap=sap: e.scalar_tensor_tensor(
                                        out=A3F.v(ch * 512, [(1, 512)]), in0=src_fn(), scalar=sap, in1=A3F.v(9216 + ch * 512, [(1, 512)]),
                                        op0=ALU.mult, op1=ALU.mult), reads=skeys + [sbk, "g4t"], writes=[("tmc", ch)])
                                    add("dve", lambda e, ch=ch, b=b, so=so: e.tensor_tensor(
                                        out=A3F.v(1024 + so * D + ch * 512, [(1, 512)]), in0=A3F.v(ch * 512, [(1, 512)]),
                                        in1=A1F.v(b * D + ch * 512, [(1, 512)]), op=ALU.add),
                                        reads=[("tmc", ch), ("x1", b)], writes=[("ot", so)])
                                outs.append(add("sp", lambda e, b=b, so=so: e.dma_start(out=out_h.ap()[b * 128:(b + 1) * 128, :], in_=OT[so]),
                                                reads=[("ot", so)], dma=("out", so)))
                return body

            items = [((lambda j=j: up_loader(j)), up_body(j)) for j in range(KF)]
            for hc in range(2):
                for j0 in range(0, KF, 4):
                    nj = min(4, KF - j0)
                    items.append(((lambda hc=hc, j0=j0, nj=nj: load_w([wrows(wd_h, hc * 512, 512)[:, j0:j0 + nj, :]], nj, 512)),
                                  down_body(hc, j0, nj)))
            run_stream(items, pre=(pre_c if half == 0 else None))
        Sd.emit(nc, final_waits=outs)
    return nc


def make_consts():
    c = np.zeros((128, 640), np.float32)
    c[:, 0:128] = np.eye(128, dtype=np.float32)
    j = np.arange(128)[:, None]
    s = np.arange(128)[None, :]
    c[:, 128:256] = np.where(j >= s, -1.0, 0.0)
    c[:, 256:384] = -1.0
    c[:, 384:512] = np.where(j >= s, NEG, 0.0)
    c[:, 512:640] = 1.0 / CONV
    return c


_NC_CACHE = {}


def kernel(**inputs):
    x = np.ascontiguousarray(np.asarray(inputs["x"], dtype=np.float32))
    B, S, _ = x.shape
    if S not in _NC_CACHE:
        _NC_CACHE[S] = build(S)
    nc = _NC_CACHE[S]
    shared = {k: np.ascontiguousarray(np.asarray(v, dtype=np.float32)) for k, v in inputs.items() if k != "x"}
    shared["consts"] = make_consts()
    in_maps = []
    for b in range(B):
        m = dict(shared)
        m["x"] = x[b]
        in_maps.append(m)
    res = run_bass_kernel_spmd(nc, in_maps, core_ids=list(range(B)))
    return np.stack([np.asarray(r["out"]) for r in res.results], axis=0).astype(np.float32)
```

```python
import contextlib
import numpy as np
import concourse.bass as bass
import concourse.mybir as mybir
from concourse.bass_utils import run_bass_kernel_spmd

F32 = mybir.dt.float32
BF16 = mybir.dt.bfloat16
AF = mybir.ActivationFunctionType
ALU = mybir.AluOpType

D = 1024
KD = 8
CONV = 512
KW = 31
PAD = KW - 1
NH = 8
DH = 64
DFF = 2816
KF = 22
INC = 4608
EPS = 1e-6
NEG = -30000.0


class Op:
    __slots__ = ("eng", "fn", "deps", "signal", "sem", "val", "dma", "idx")

    def __init__(self, eng, fn, dma):
        self.eng = eng
        self.fn = fn
        self.dma = dma
        self.deps = set()
        self.signal = False
        self.sem = None
        self.val = 0


class Sched:
    ENG = ("pe", "act", "dve", "pool", "sp")

    def __init__(self):
        self.ops = {e: [] for e in self.ENG}
        self.last_w = {}
        self.readers = {}
        self.n = 0
        self.pending = {e: set() for e in self.ENG}

    def add(self, eng, fn, reads=(), writes=(), dma=None):
        op = Op(eng, fn, dma)
        op.idx = self.n
        self.n += 1
        deps = set(self.pending[eng])
        self.pending[eng] = set()
        for k in reads:
            w = self.last_w.get(k)
            if w is not None:
                deps.add(w)
            if isinstance(k, tuple) and k and k[0] == "ps":
                deps.update(r for r in self.readers.get(k, ()) if r.eng != eng)
        for k in writes:
            w = self.last_w.get(k)
            if w is not None:
                deps.add(w)
            deps.update(self.readers.get(k, ()))
        if eng == "pe":
            deps = {d for d in deps if not (d.eng == "pe" and d.dma is None)}
        newest = {}
        rest = set()
        for d in deps:
            if d.dma is None and d.eng in ("pe", "act", "dve"):
                if d.eng not in newest or newest[d.eng].idx < d.idx:
                    newest[d.eng] = d
            else:
                rest.add(d)
        deps = rest | set(newest.values())
        op.deps = deps
        for d in deps:
            d.signal = True
        for k in reads:
            self.readers.setdefault(k, []).append(op)
        for k in writes:
            self.last_w[k] = op
            self.readers[k] = []
        self.ops[eng].append(op)
        return op

    def barrier(self):
        lasts = set()
        for e in self.ENG:
            if self.ops[e]:
                lasts.add(self.ops[e][-1])
            for o in self.ops[e]:
                if o.dma is not None:
                    lasts.add(o)
        newest = {}
        keep = set()
        for o in lasts:
            if o.dma is None:
                keep.add(o)
            else:
                if o.dma not in newest or newest[o.dma].idx < o.idx:
                    newest[o.dma] = o
        keep.update(newest.values())
        for e in self.ENG:
            self.pending[e] = set(keep)

    def emit(self, nc, final_waits=()):
        with contextlib.ExitStack() as es:
            esem = {e: es.enter_context(nc.semaphore("s_" + e)) for e in self.ENG}
            dsem = {}
            dcnt = {}
            allops = []
            for e in self.ENG:
                allops.extend(self.ops[e])
            allops.sort(key=lambda o: o.idx)
            for op in allops:
                if op.dma is not None:
                    if op.dma not in dsem:
                        dsem[op.dma] = es.enter_context(nc.semaphore("d_%d" % len(dsem)))
                        dcnt[op.dma] = 0
                    dcnt[op.dma] += 16
                    op.sem = dsem[op.dma]
                    op.val = dcnt[op.dma]
            for e in self.ENG:
                c = 0
                for op in self.ops[e]:
                    if op.dma is None and op.signal:
                        c += 1
                        op.sem = esem[e]
                        op.val = c
            sched = self

            def run(eng_name, eng):
                waited = {}
                for op in sched.ops[eng_name]:
                    for d in sorted(op.deps, key=lambda o: o.idx):
                        sid = id(d.sem)
                        if waited.get(sid, 0) >= d.val:
                            continue
                        eng.wait_ge(d.sem, d.val)
                        waited[sid] = d.val
                    ins = op.fn(eng)
                    if op.dma is not None:
                        ins.then_inc(op.sem, 16)
                    elif op.signal:
                        ins.then_inc(op.sem, 1)
                if eng_name == "sp":
                    done = {}
                    for o in final_waits:
                        if done.get(id(o.sem), (None, 0))[1] < o.val:
                            done[id(o.sem)] = (o.sem, o.val)
                    for sem, val in done.values():
                        eng.wait_ge(sem, val)

            with nc.Block() as block:

                @block.tensor
                def _(e):
                    run("pe", e)

                @block.scalar
                def _(e):
                    run("act", e)

                @block.vector
                def _(e):
                    run("dve", e)

                @block.gpsimd
                def _(e):
                    run("pool", e)

                @block.sync
                def _(e):
                    run("sp", e)


class Buf:
    def __init__(self, t, w):
        self.t = t
        self.w = w

    def v(self, off, dims, p0=0, npart=128):
        return bass.AP(self.t, p0 * self.w + off, [[self.w, npart]] + [[s, c] for (s, c) in dims])


def build(S, dbg=False):
    NT = S // 512
    NB = S // 128
    nc = bass.Bass("TRN2", target_bir_lowering=False)
    dram = lambda name, shape: nc.dram_tensor(name, shape, F32, kind="ExternalInput")
    x_h = dram("x", [S, D])
    g1_h = dram("norm_mix_pre", [D])
    win_h = dram("w_in", [D, INC])
    cw_h = dram("conv_dw_w", [KW, CONV])
    cb_h = dram("conv_dw_b", [CONV])
    lg_h = dram("conv_ln_g", [CONV])
    lb_h = dram("conv_ln_b", [CONV])
    wc_h = dram("w_conv_branch", [CONV, D])
    bc_h = dram("b_conv_branch", [D])
    wa_h = dram("w_att_branch", [CONV, D])
    wo_h = dram("w_out", [D, D])
    g2_h = dram("norm_mix_post", [D])
    g3_h = dram("norm_ffn_pre", [D])
    wu_h = dram("w_ffn_up", [D, 2 * DFF])
    wd_h = dram("w_ffn_down", [DFF, D])
    g4_h = dram("norm_ffn_post", [D])
    cst_h = dram("consts", [128, 640])
    out_h = nc.dram_tensor("out", [S, D], F32, kind="ExternalOutput")

    es = contextlib.ExitStack()
    with es:
        def sb(name, w, dt):
            return Buf(es.enter_context(nc.sbuf_tensor(name, [128, w], dt)), w)

        US = PAD + S
        UREG = max(4 * US, 8192)
        A1W = UREG + 12 * S
        ar1 = es.enter_context(nc.sbuf_tensor("ar1", [128, A1W], BF16))
        A1 = Buf(ar1, A1W)
        A1F = Buf(ar1.bitcast(F32), A1W // 2)
        A2W = 16 * S
        ar2 = es.enter_context(nc.sbuf_tensor("ar2", [128, A2W], BF16))
        A2 = Buf(ar2, A2W)
        A2F = Buf(ar2.bitcast(F32), A2W // 2)
        A3W = 20480
        ar3 = es.enter_context(nc.sbuf_tensor("ar3", [128, A3W], BF16))
        A3 = Buf(ar3, A3W)
        A3F = Buf(ar3.bitcast(F32), A3W // 2)
        wst = [sb("wst%d" % i, 2048, F32) for i in range(2)]
        wbf = [sb("wbf%d" % i, 2048, BF16) for i in range(2)]
        cstf = sb("cstf", 640, F32)
        cstb = sb("cstb", 640, BF16)
        cols = sb("cols", 64, F32)
        stat = sb("stat", 5 * NB + 8, F32)
        cwb = sb("cwb", CONV, BF16)
        cwT = sb("cwT", 128, F32)
        g13 = sb("g13", D, F32)

        C_G1, C_G3, C_CB, C_LG, C_LB, C_BC = 0, 8, 16, 20, 24, 28
        uT_off, qT_off, kT_off, v_off = 0, UREG, UREG + 4 * S, UREG + 8 * S
        hT_off, aT_off, u2_off = 0, 8 * S, 12 * S
        IDN, TRI, ONE, MSK, ONM = 0, 128, 256, 384, 512

        XN_O, JUNK_O = 16384, 17408
        XN = A3.v(XN_O, [(1, D)])
        JUNK = A3.v(JUNK_O, [(1, D)])

        ps = [Buf(es.enter_context(nc.psum_tensor("ps%d" % i, [128, 1024], F32)), 1024) for i in range(4)]
        psb = [Buf(p.t.bitcast(BF16), 2048) for p in ps]

        def bank(i):
            return ps[i // 2], (i % 2) * 512

        Sd = Sched()
        add = Sd.add

        add("sp", lambda e: e.dma_start(out=cstf.v(0, [(1, 640)]), in_=cst_h.ap()), writes=["cstf"], dma="c0")
        add("dve", lambda e: e.tensor_copy(out=cstb.v(0, [(1, 640)]), in_=cstf.v(0, [(1, 640)])), reads=["cstf"], writes=["cstb"])
        add("dve", lambda e: e.memset(stat.v(0, [(1, 5 * NB + 8)]), 0.0), writes=[("stat", i) for i in range(5 * NB)])

        def colvec(h, n, slot):
            add("sp", lambda e: e.dma_start(out=cols.v(slot, [(1, n)]), in_=h.ap().rearrange("(j p) -> p j", p=128),
                                            allow_slow_non_contiguous=True),
                writes=[("cols", slot)], dma=("cols", slot))

        colvec(cb_h, 4, C_CB)
        colvec(lg_h, 4, C_LG)
        colvec(lb_h, 4, C_LB)
        colvec(bc_h, 8, C_BC)

        def bcast_row(h):
            return bass.AP(h, 0, [[0, 128], [1, D]])

        G13 = g13.v(0, [(1, D)])
        add("sp", lambda e: e.dma_start(out=G13, in_=bcast_row(g1_h)), writes=["g13"], dma="g13")

        ident = lambda: cstb.v(IDN, [(1, 128)])
        trineg = lambda: cstb.v(TRI, [(1, 128)])
        onesneg = lambda: cstb.v(ONE, [(1, 128)])
        maskneg = lambda: cstb.v(MSK, [(1, 128)])
        onesmean = lambda: cstb.v(ONM, [(1, 128)])

        wslot = [0]
        ncast = [0]

        def load_w(srcs, kch, ncols, gslot=None, cast_eng=None):
            s = wslot[0] % 2
            wslot[0] += 1
            c0 = 0
            pkeys = []
            for pi, src in enumerate(srcs):
                n_i = src.shape[-1]
                add("sp", (lambda s=s, c0=c0, n_i=n_i, src=src: lambda e: e.dma_start(
                    out=wst[s].v(c0, [(ncols, kch), (1, n_i)]), in_=src))(),
                    writes=[("wst", s, pi)], dma=("wst", s, pi))
                pkeys.append(("wst", s, pi))
                c0 += n_i
            allk = [("wst", s, 0), ("wst", s, 1)]
            if gslot is None:
                ce = cast_eng or ("act" if (ncast[0] % 2 == 0) else "dve")
                ncast[0] += 1
                if ce == "act":
                    add("act", lambda e, s=s: e.copy(out=wbf[s].v(0, [(1, kch * ncols)]), in_=wst[s].v(0, [(1, kch * ncols)])),
                        reads=allk, writes=[("wbf", s)])
                else:
                    add("dve", lambda e, s=s: e.tensor_copy(out=wbf[s].v(0, [(1, kch * ncols)]), in_=wst[s].v(0, [(1, kch * ncols)])),
                        reads=allk, writes=[("wbf", s)])
            else:
                for k in range(kch):
                    add("dve", lambda e, s=s, k=k: e.tensor_scalar(
                        out=wbf[s].v(k * ncols, [(1, ncols)]), in0=wst[s].v(k * ncols, [(1, ncols)]),
                        scalar1=cols.v(gslot + k, [(1, 1)]), scalar2=None, op0=ALU.mult),
                        reads=allk + [("cols", gslot)], writes=[("wbf", s)])
            return s

        def load_w_dma(srcs, kch, ncols):
            s_ = wslot[0] % 2
            wslot[0] += 1
            c0 = 0
            for pi, src in enumerate(srcs):
                n_i = src.shape[-1]
                add("sp", (lambda s_=s_, c0=c0, n_i=n_i, src=src: lambda e: e.dma_start(
                    out=wst[s_].v(c0, [(ncols, kch), (1, n_i)]), in_=src))(),
                    writes=[("wst", s_, pi)], dma=("wst", s_, pi))
                c0 += n_i
            return s_

        def load_w_cast(s_, kch, ncols):
            add("dve", lambda e: e.tensor_copy(out=wbf[s_].v(0, [(1, kch * ncols)]), in_=wst[s_].v(0, [(1, kch * ncols)])),
                reads=[("wst", s_, 0), ("wst", s_, 1)], writes=[("wbf", s_)])

        def run_stream(items, pre=None, after_first=None):
            nxt = items[0][0]() if pre is None else pre
            for i, (ld, body) in enumerate(items):
                cur = nxt
                if i + 1 < len(items):
                    nxt = items[i + 1][0]()
                body(cur)
                if i == 0 and after_first is not None:
                    after_first()

        def wrows(h, c0, n):
            return h.ap().rearrange("(k p) c -> p k c", p=128)[:, :, c0:c0 + n]

        def mm(out, lhsT, rhs, start, stop, reads, writes):
            add("pe", lambda e: e.matmul(out, lhsT=lhsT, rhs=rhs, start=start, stop=stop), reads=reads, writes=writes)

        def rstd_from(ap, scale, key):
            add("act", lambda e: e.activation(out=ap, in_=ap, func=AF.Ln, scale=scale, bias=EPS), reads=[key], writes=[key])
            add("act", lambda e: e.activation(out=ap, in_=ap, func=AF.Exp, scale=-0.5), reads=[key], writes=[key])

        XN2_O = 15360
        XNB = [(XN_O, "xn0"), (XN2_O, "xn1")]

        def norm_transpose_seq(blocks, pbase=6):
            def s1(i):
                pre, src_fn, src_key, scol, dst_ap, dst_key = blocks[i]
                if pre is not None:
                    pre()
                skey = ("stat", scol)
                sap = stat.v(scol, [(1, 1)])
                add("act", lambda e: e.activation(out=JUNK, in_=src_fn(), func=AF.Square, accum_out=sap),
                    reads=[src_key, skey], writes=["junk", skey])
                rstd_from(sap, 1.0 / D, skey)

            def s2(i):
                pre, src_fn, src_key, scol, dst_ap, dst_key = blocks[i]
                skey = ("stat", scol)
                sap = stat.v(scol, [(1, 1)])
                xo, xk = XNB[i % 2]
                pb = pbase + (i % 2)
                add("dve", lambda e: e.scalar_tensor_tensor(out=A3.v(xo, [(1, D)]), in0=src_fn(), scalar=sap, in1=G13, op0=ALU.mult, op1=ALU.mult),
                    reads=[src_key, skey, "g13"], writes=[xk])
                PB = psb[pb // 2]
                ob = (pb % 2) * 1024
                for k in range(KD):
                    add("pe", lambda e, k=k: e.transpose(out=PB.v(ob + k * 128, [(1, 128)]),
                                                         in_=A3.v(xo + k * 128, [(1, 128)]), identity=ident()),
                        reads=[xk, "cstb"], writes=[("ps", pb)])

            def s3(i):
                pre, src_fn, src_key, scol, dst_ap, dst_key = blocks[i]
                pb = pbase + (i % 2)
                PB = psb[pb // 2]
                ob = (pb % 2) * 1024
                add("dve", lambda e: e.tensor_copy(out=dst_ap, in_=PB.v(ob, [(128, KD), (1, 128)])),
                    reads=[("ps", pb)], writes=[dst_key])

            s1(0)
            for i in range(len(blocks)):
                if i + 1 < len(blocks):
                    s1(i + 1)
                s2(i)
                if i >= 1:
                    s3(i - 1)
            s3(len(blocks) - 1)

        XS = [A3F.v(0, [(1, D)]), A3F.v(D, [(1, D)]), A3F.v(3072, [(1, D)]), A3F.v(4096, [(1, D)])]

        def xload(b):
            s_ = b % 4
            return lambda: add("sp", lambda e: e.dma_start(out=XS[s_], in_=x_h.ap()[b * 128:(b + 1) * 128, :]),
                               writes=[("xs", s_)], dma=("xs", s_))
        norm_transpose_seq([(xload(b), (lambda b=b: XS[b % 4]), ("xs", b % 4), b,
                             A2.v(hT_off + b * 128, [(S, KD), (1, 128)]), ("hT", b // 4)) for b in range(NB)])

        add("dve", lambda e: e.memset(A1.v(uT_off, [(US, 4), (1, PAD)]), 0.0), writes=[("uT", -1)])
        SG = A3F.v(2048, [(1, 512)])
        def a1_body(j):
            def body(s):
                for t in range(NT):
                    ba, bg = 2 * (t % 2), 2 * (t % 2) + 1
                    Pa, oa = bank(ba)
                    Pg, og = bank(bg)
                    for k in range(KD):
                        mm(Pa.v(oa, [(1, 512)]), wbf[s].v(k * 256, [(1, 128)]), A2.v(hT_off + k * S + t * 512, [(1, 512)]),
                           k == 0, k == KD - 1, [("wbf", s), ("hT", t)], [("ps", ba)])
                    for k in range(KD):
                        mm(Pg.v(og, [(1, 512)]), wbf[s].v(k * 256 + 128, [(1, 128)]), A2.v(hT_off + k * S + t * 512, [(1, 512)]),
                           k == 0, k == KD - 1, [("wbf", s), ("hT", t)], [("ps", bg)])
                    add("act", lambda e, Pg=Pg, og=og: e.activation(out=SG, in_=Pg.v(og, [(1, 512)]), func=AF.Sigmoid),
                        reads=[("ps", bg)], writes=["sg"])
                    add("dve", lambda e, Pa=Pa, oa=oa, t=t: e.tensor_tensor(
                        out=A1.v(uT_off + j * US + PAD + t * 512, [(1, 512)]), in0=Pa.v(oa, [(1, 512)]), in1=SG, op=ALU.mult),
                        reads=[("ps", ba), "sg"], writes=[("uT", j, t)])
            return body
        run_stream([((lambda j=j: load_w([wrows(win_h, 128 * j, 128), wrows(win_h, 512 + 128 * j, 128)], KD, 256)), a1_body(j))
                    for j in range(4)])

        cwf = A3F.v(2560, [(1, CONV)], npart=KW)
        add("sp", lambda e: e.dma_start(out=cwf, in_=cw_h.ap()), writes=["cwf"], dma="cw")
        add("dve", lambda e: e.tensor_copy(out=cwb.v(0, [(1, CONV)], npart=KW), in_=cwf), reads=["cwf"], writes=["cwb"])
        for j in range(4):
            add("pe", lambda e, j=j: e.transpose(out=psb[3].v(j * 32, [(1, KW)]), in_=cwb.v(j * 128, [(1, 128)], npart=KW),
                                                 identity=cstb.v(IDN, [(1, KW)], npart=KW)),
                reads=["cwb", "cstb"], writes=[("ps", 6)])
        add("dve", lambda e: e.tensor_copy(out=cwT.v(0, [(32, 4), (1, KW)]), in_=psb[3].v(0, [(32, 4), (1, KW)])),
            reads=[("ps", 6)], writes=["cwT"])
        Y_off = qT_off // 2
        YRING = (NT > 2)
        assert (not YRING) or 8 * 512 * 2 <= 4 * S

        def Yv(j, t):
            if YRING:
                return A1F.v(kT_off // 2 + ((t % 2) * 4 + j) * 512, [(1, 512)])
            return A1F.v(Y_off + j * S + t * 512, [(1, 512)])
        ykey = lambda j, t: ("y", j, (t % 2) if YRING else t)
        DG = lambda k: A3.v(6144 + k * 128, [(1, 128)])
        YB = lambda j: A3.v(10112 + j * 512, [(1, 512)])
        YQ = lambda j: A3.v(12160 + j * 512, [(1, 512)])
        MU = A3F.v(7104, [(1, 512)])
        RS = A3F.v(7616, [(1, 512)])
        T1 = A3F.v(9216, [(1, 512)])
        def ln_tile(t):
            for j in range(4):
                add("dve", lambda e, j=j, t=t: e.tensor_copy(out=YB(j), in_=Yv(j, t)),
                    reads=[ykey(j, t)], writes=[("yb", j)])
                add("act", lambda e, j=j, t=t: e.activation(out=YQ(j), in_=Yv(j, t), func=AF.Square),
                    reads=[ykey(j, t)], writes=[("yq", j)])
            Pm, om = bank(4)
            Pq, oq = bank(5)
            for j in range(4):
                mm(Pm.v(om, [(1, 512)]), onesmean(), YB(j), j == 0, j == 3, ["cstb", ("yb", j)], [("ps", 4)])
            for j in range(4):
                mm(Pq.v(oq, [(1, 512)]), onesmean(), YQ(j), j == 0, j == 3, ["cstb", ("yq", j)], [("ps", 5)])
            add("act", lambda e, Pm=Pm, om=om: e.copy(out=MU, in_=Pm.v(om, [(1, 512)])), reads=[("ps", 4)], writes=["mu"])
            add("dve", lambda e: e.tensor_tensor(out=T1, in0=MU, in1=MU, op=ALU.mult), reads=["mu"], writes=["t1"])
            add("dve", lambda e, Pq=Pq, oq=oq: e.tensor_tensor(out=RS, in0=Pq.v(oq, [(1, 512)]), in1=T1, op=ALU.subtract),
                reads=[("ps", 5), "t1"], writes=["rs"])
            rstd_from(RS, 1.0, "rs")
            for j in range(4):
                add("dve", lambda e, j=j, t=t: e.tensor_tensor(out=T1, in0=Yv(j, t), in1=MU, op=ALU.subtract),
                    reads=[ykey(j, t), "mu"], writes=["t1"])
                add("dve", lambda e: e.tensor_tensor(out=T1, in0=T1, in1=RS, op=ALU.mult), reads=["t1", "rs"], writes=["t1"])
                add("act", lambda e, j=j, t=t: e.activation(out=A2.v(u2_off + j * S + t * 512, [(1, 512)]), in_=T1, func=AF.Silu,
                                                            scale=cols.v(C_LG + j, [(1, 1)]), bias=cols.v(C_LB + j, [(1, 1)])),
                    reads=["t1", ("cols", C_LG), ("cols", C_LB)], writes=[("u2T", j, t)])


        DGSZ = KW * 128
        if 4 * S >= 2 * DGSZ:
            dg_a1 = {2: v_off, 3: v_off + DGSZ}
        else:
            assert UREG - 4 * US >= DGSZ and 4 * S >= DGSZ
            dg_a1 = {2: v_off, 3: 4 * US}

        def DGA(j, k):
            if j < 2:
                return A3.v(j * DGSZ + k * 128, [(1, 128)])
            return A1.v(dg_a1[j] + k * 128, [(1, 128)])
        assert 2 * DGSZ <= 10112
        for j in range(4):
            for k in range(KW):
                add("dve", lambda e, j=j, k=k: e.tensor_scalar(out=DGA(j, k), in0=ident(), scalar1=cwT.v(j * 32 + k, [(1, 1)]),
                                                               scalar2=None, op0=ALU.mult),
                    reads=["cstb", "cwT"], writes=[("dg", j, k)])
        steps = [(t, j) for t in range(NT) for j in range(4)]

        def conv_step(i):
            t, j = steps[i]
            by = 2 + (i % 2)
            Py, oy = bank(by)
            for k in range(KW):
                rk = [("dg", j, k), ("uT", j, t), ("uT", -1)] + ([("uT", j, t - 1)] if t > 0 else [])
                mm(Py.v(oy, [(1, 512)]), DGA(j, k), A1.v(uT_off + j * US + t * 512 + k, [(1, 512)]), k == 0, k == KW - 1, rk, [("ps", by)])
            add("dve", lambda e: e.tensor_scalar(
                out=Yv(j, t), in0=Py.v(oy, [(1, 512)]),
                scalar1=cols.v(C_CB + j, [(1, 1)]), scalar2=None, op0=ALU.add),
                reads=[("ps", by), ("cols", C_CB)], writes=[ykey(j, t)])

        for i, (t, j) in enumerate(steps):
            conv_step(i)
            if j == 0 and t >= 1:
                ln_tile(t - 1)
        if not YRING:
            ln_tile(NT - 1)
        pre_a3 = load_w([wrows(win_h, 1024, 256)], KD, 256)
        if not YRING:
            Sd.barrier()
        Y_ALIAS = [("y", j, r) for j in range(4) for r in range(2)] if YRING else []
        DG_ALIAS = [("dg", j, k) for j in (2, 3) for k in range(KW)]
        UT_ALIAS = [("uT", j, t) for j in range(4) for t in range(NT)] + [("uT", -1)]

        def qk_body(which, g, dst_off):
            def body(s):
                for cc in range(2):
                    c = 2 * g + cc
                    for t in range(NT):
                        bq = (2 * c + t) % 4
                        Pq_, oq_ = bank(bq)
                        for k in range(KD):
                            mm(Pq_.v(oq_, [(1, 512)]), wbf[s].v(k * 256 + cc * 128, [(1, 128)]), A2.v(hT_off + k * S + t * 512, [(1, 512)]),
                               k == 0, k == KD - 1, [("wbf", s), ("hT", t)], [("ps", bq)])
                        if which == "q":
                            add("act", lambda e, P=Pq_, o=oq_, c=c, t=t: e.activation(
                                out=A1.v(dst_off + c * S + t * 512, [(1, 512)]), in_=P.v(o, [(1, 512)]), func=AF.Copy, scale=0.125),
                                reads=[("ps", bq)], writes=[("qT", c, t)])
                        else:
                            add("dve", lambda e, P=Pq_, o=oq_, c=c, t=t: e.tensor_copy(
                                out=A1.v(dst_off + c * S + t * 512, [(1, 512)]), in_=P.v(o, [(1, 512)])),
                                reads=[("ps", bq)], writes=[("kT", c)] + Y_ALIAS)
            return body

        def v_body(g):
            def body(s):
                for b in range(NB):
                    bv = 4 + (b % 4)
                    Pv, ov = bank(bv)
                    for k in range(KD):
                        mm(Pv.v(ov, [(1, 256)]), A2.v(hT_off + k * S + b * 128, [(1, 128)]), wbf[s].v(k * 256, [(1, 256)]),
                           k == 0, k == KD - 1, [("wbf", s), ("hT", b // 4)], [("ps", bv)])
                    add("dve", lambda e, Pv=Pv, ov=ov, b=b: e.tensor_copy(out=A1.v(v_off + b * 512 + 256 * g, [(1, 256)]), in_=Pv.v(ov, [(1, 256)])),
                        reads=[("ps", bv)], writes=[("v", b, g)] + DG_ALIAS)
            return body

        items = []
        for which, base_col, dst_off in (("q", 1024, qT_off), ("k", 1536, kT_off)):
            for g in range(2):
                items.append(((lambda base_col=base_col, g=g: load_w([wrows(win_h, base_col + 256 * g, 256)], KD, 256)), qk_body(which, g, dst_off)))
        for g in range(2):
            items.append(((lambda g=g: load_w([wrows(win_h, 2048 + 256 * g, 256)], KD, 256)), v_body(g)))
        run_stream(items, pre=pre_a3, after_first=((lambda: ln_tile(NT - 1)) if YRING else None))

        WCo, WAo = 0, 4096
        wcwa_jobs = [(h_, off_, g) for (h_, off_) in ((wc_h, WCo), (wa_h, WAo)) for g in range(4)]
        wcwa_slot = {}

        def wcwa_dma(n):
            h_, off_, g = wcwa_jobs[n]
            wcwa_slot[n] = load_w_dma([wrows(h_, 256 * g, 256)], 4, 256)

        def wcwa_fin(n):
            h_, off_, g = wcwa_jobs[n]
            sl = wcwa_slot[n]
            load_w_cast(sl, 4, 256)
            add("dve", lambda e: e.tensor_copy(out=A1.v(off_ + 256 * g, [(D, 4), (1, 256)]), in_=wbf[sl].v(0, [(256, 4), (1, 256)])),
                reads=[("wbf", sl)], writes=[("wcwa", off_)] + UT_ALIAS)

        Sd.barrier()
        Eo = [0, 2048]
        SPo = [4096, 5120, 6144]
        SCo = [7168, 8192]
        ATo = [9216, 10240]
        ZR = A3.v(11264, [(1, 64)])
        add("dve", lambda e: e.memset(ZR, 0.0), writes=["zr"])

        def v3(off, c0, n):
            return A3.v(off + c0, [(512, 2), (1, n)])

        def e3(i, c0, n):
            return A3F.v(Eo[i] // 2 + c0, [(512, 2), (1, n)])

        units = []
        for qt in range(NT):
            for c in range(4):
                seq = [(4 * qt + r, 128 * r, True) for r in (3, 2, 1, 0)] + [(kb, 0, False) for kb in range(4 * qt - 1, -1, -1)]
                for i, (kb, c0, dg) in enumerate(seq):
                    units.append(dict(qt=qt, c=c, kb=kb, c0=c0, diag=dg, first=(i == 0), last=(i == len(seq) - 1),
                                      pc0=(seq[i - 1][1] if i > 0 else None), g=qt * 4 + c))
        NU = len(units)
        zk = lambda u: [("ps", 2 * (u % 2)), ("ps", 2 * (u % 2) + 1)]
        ak = [("ps", 4), ("ps", 5)]
        Zv = lambda u, hh, c0, n: ps[u % 2].v(hh * 512 + c0, [(1, n)])
        Z2 = lambda u, c0, n: ps[u % 2].v(c0, [(512, 2), (1, n)])
        Av = lambda hh, c0, n: ps[2].v(hh * 512 + c0, [(1, n)])
        A2v = lambda c0, n: ps[2].v(c0, [(512, 2), (1, n)])

        def zmm(u, out_fn, keys, close):
            U = units[u]
            n = 512 - U["c0"]
            for hh in range(2):
                mm(out_fn(hh, U["c0"], n),
                   A1.v(kT_off + U["c"] * S + U["kb"] * 128, [(1, 128)], p0=hh * 64, npart=64),
                   A1.v(qT_off + U["c"] * S + U["qt"] * 512 + U["c0"], [(1, n)], p0=hh * 64, npart=64),
                   True, close and not U["diag"], [("kT", U["c"]), ("qT", U["c"], U["qt"])], keys)
                if U["diag"]:
                    mm(out_fn(hh, U["c0"], 128), ident(), maskneg(), False, close, ["cstb"], keys)

        def stA(u):
            zmm(u, lambda hh, c0, n: Zv(u, hh, c0, n), zk(u), True)

        def stB(u):
            U = units[u]
            c0 = U["c0"]
            n = 512 - c0
            add("act", lambda e: e.activation(out=e3(u % 2, c0, n), in_=Z2(u, c0, n), func=AF.Exp),
                reads=zk(u), writes=[("e", u % 2)])
            add("act", lambda e: e.activation(out=v3(SPo[u % 3], c0, n), in_=e3(u % 2, c0, n), func=AF.Ln, bias=1.0),
                reads=[("e", u % 2)], writes=[("sp", u % 3)])

        def stC(u):
            U = units[u]
            c0 = U["c0"]
            n = 512 - c0
            pc0 = U["pc0"]
            if not U["last"]:
                if U["first"]:
                    add("dve", lambda e: e.tensor_copy(out=v3(SCo[u % 2], c0, n), in_=v3(SPo[u % 3], c0, n)),
                        reads=[("sp", u % 3)], writes=[("sc", u % 2)])
                else:
                    if pc0 > c0:
                        add("dve", lambda e: e.tensor_copy(out=v3(SCo[u % 2], c0, pc0 - c0), in_=v3(SPo[u % 3], c0, pc0 - c0)),
                            reads=[("sp", u % 3)], writes=[("sc", u % 2)])
                    add("dve", lambda e: e.tensor_tensor(out=v3(SCo[u % 2], pc0, 512 - pc0), in0=v3(SPo[u % 3], pc0, 512 - pc0),
                                                         in1=v3(SCo[(u - 1) % 2], pc0, 512 - pc0), op=ALU.add),
                        reads=[("sp", u % 3), ("sc", (u - 1) % 2)], writes=[("sc", u % 2)])
            zmm(u, lambda hh, c0_, n_: Av(hh, c0_, n_), ak, False)
            for hh in range(2):
                mm(Av(hh, c0, n), trineg(), A3.v(SPo[u % 3] + hh * 512 + c0, [(1, n)]), False, U["first"],
                   ["cstb", ("sp", u % 3)], ak)
                if not U["first"]:
                    mm(Av(hh, pc0, 512 - pc0), onesneg(), A3.v(SCo[(u - 1) % 2] + hh * 512 + pc0, [(1, 512 - pc0)]), False, True,
                       ["cstb", ("sc", (u - 1) % 2)], ak)

        def stD(u):
            U = units[u]
            c0 = U["c0"]
            n = 512 - c0
            add("act", lambda e: e.activation(out=v3(ATo[u % 2], c0, n), in_=A2v(c0, n), func=AF.Exp),
                reads=ak, writes=[("at", u % 2)])

        def stE(u):
            U = units[u]
            c0 = U["c0"]
            n = 512 - c0
            g = U["g"]
            ob = 6 + (g % 2)
            Po, oo = bank(ob)
            if U["first"]:
                for hh in range(2):
                    mm(Po.v(oo, [(1, 512)], p0=hh * 64, npart=64), ZR, cstb.v(0, [(1, 512)]), True, False,
                       ["zr", "cstb"], [("ps", ob)])
            for hh in range(2):
                mm(Po.v(oo + c0, [(1, n)], p0=hh * 64, npart=64),
                   A1.v(v_off + U["kb"] * 512 + (2 * U["c"] + hh) * 64, [(1, 64)]),
                   A3.v(ATo[u % 2] + hh * 512 + c0, [(1, n)]), False, U["last"],
                   [("v", U["kb"], 0), ("v", U["kb"], 1), ("at", u % 2)], [("ps", ob)])
            if U["last"]:
                add("dve", lambda e: e.tensor_copy(out=A2.v(aT_off + U["c"] * S + U["qt"] * 512, [(1, 512)]), in_=Po.v(oo, [(1, 512)])),
                    reads=[("ps", ob)], writes=[("attT", U["c"], U["qt"])])

        stA(0)
        if NU > 1:
            stA(1)
        stB(0)
        stC(0)
        assert NU >= 4 * (len(wcwa_jobs) + 1)
        for u in range(NU):
            if u % 4 == 0:
                n = u // 4
                if n < len(wcwa_jobs):
                    wcwa_dma(n)
                if 1 <= n <= len(wcwa_jobs):
                    wcwa_fin(n - 1)
            if u >= 1:
                stE(u - 1)
            if u + 2 < NU:
                stA(u + 2)
            if u + 1 < NU:
                stB(u + 1)
            stD(u)
            if u + 1 < NU:
                stC(u + 1)
        stE(NU - 1)

        gate_loader = lambda c: load_w([wrows(win_h, 2560 + 128 * c, 128), wrows(win_h, 3584 + 128 * c, 128)], KD, 256)
        pre_b1 = gate_loader(0)
        G4T = A3F.v(9216, [(1, D)])
        add("sp", lambda e: e.dma_start(out=G4T, in_=bcast_row(g4_h)), writes=["g4t"], dma="g4t")
        add("sp", lambda e: e.dma_start(out=G13, in_=bcast_row(g3_h)), writes=["g13"], dma="g13")
        Sd.barrier()

        MT = lambda c, t: A3.v(c * S + t * 512, [(1, 512)])
        GS0 = (qT_off + 1) // 2 + 8
        SG1 = A1F.v(GS0, [(1, 512)])
        SG2 = A1F.v(GS0 + 512, [(1, 512)])
        M1 = A1F.v(GS0 + 1024, [(1, 512)])
        M2 = A1F.v(GS0 + 1536, [(1, 512)])
        items = []

        def b1_body(c):
            def body(s1):
                for t in range(NT):
                    b0 = 4 * (t % 2)
                    Pc, oc = bank(b0)
                    Pa_, oa_ = bank(b0 + 1)
                    Pg1, og1 = bank(b0 + 2)
                    Pg2, og2 = bank(b0 + 3)
                    for k in range(4):
                        mm(Pc.v(oc, [(1, 512)]), A1.v(WCo + k * D + 128 * c, [(1, 128)]), A2.v(u2_off + k * S + t * 512, [(1, 512)]),
                           k == 0, k == 3, [("wcwa", WCo), ("u2T", k, t)], [("ps", b0)])
                    for k in range(4):
                        mm(Pa_.v(oa_, [(1, 512)]), A1.v(WAo + k * D + 128 * c, [(1, 128)]), A2.v(aT_off + k * S + t * 512, [(1, 512)]),
                           k == 0, k == 3, [("wcwa", WAo), ("attT", k, t)], [("ps", b0 + 1)])
                    for k in range(KD):
                        mm(Pg1.v(og1, [(1, 512)]), wbf[s1].v(k * 256, [(1, 128)]), A2.v(hT_off + k * S + t * 512, [(1, 512)]),
                           k == 0, k == KD - 1, [("wbf", s1), ("hT", t)], [("ps", b0 + 2)])
                    for k in range(KD):
                        mm(Pg2.v(og2, [(1, 512)]), wbf[s1].v(k * 256 + 128, [(1, 128)]), A2.v(hT_off + k * S + t * 512, [(1, 512)]),
                           k == 0, k == KD - 1, [("wbf", s1), ("hT", t)], [("ps", b0 + 3)])
                    add("act", lambda e, P=Pg1, o=og1: e.activation(out=SG1, in_=P.v(o, [(1, 512)]), func=AF.Sigmoid),
                        reads=[("ps", b0 + 2)], writes=["sg1"])
                    add("act", lambda e, P=Pg2, o=og2: e.activation(out=SG2, in_=P.v(o, [(1, 512)]), func=AF.Sigmoid),
                        reads=[("ps", b0 + 3)], writes=["sg2"])
                    add("dve", lambda e, P=Pc, o=oc: e.scalar_tensor_tensor(
                        out=M1, in0=P.v(o, [(1, 512)]), scalar=cols.v(C_BC + c, [(1, 1)]), in1=SG1,
                        op0=ALU.add, op1=ALU.mult), reads=[("ps", b0), "sg1", ("cols", C_BC)], writes=["m1"])
                    add("dve", lambda e, P=Pa_, o=oa_: e.tensor_tensor(out=M2, in0=P.v(o, [(1, 512)]), in1=SG2, op=ALU.mult),
                        reads=[("ps", b0 + 1), "sg2"], writes=["m2"])
                    add("dve", lambda e, t=t: e.tensor_tensor(out=MT(c, t), in0=M1, in1=M2, op=ALU.add),
                        reads=["m1", "m2"], writes=[("mT", c, t)])
            return body
        for c in range(8):
            items.append(((lambda c=c: gate_loader(c)), b1_body(c)))
        wo_loader = lambda g: load_w([wrows(wo_h, 256 * g, 256)], KD, 256)
        run_stream(items, pre=pre_b1)
        pre_b2 = wo_loader(0)

        Sd.barrier()

        WO = lambda k, c0, n: A2.v(k * D + c0, [(1, n)])
        XR = [A2F.v(4096 + i * D, [(1, D)]) for i in range(2)]
        TMB = A2F.v(6144, [(1, D)])
        G2T = A2F.v(7168, [(1, D)])
        add("sp", lambda e: e.dma_start(out=G2T, in_=bcast_row(g2_h)), writes=["g2t"], dma="g2t")
        def wo_body(g):
            def body(sl):
                add("dve", lambda e: e.tensor_copy(out=A2.v(256 * g, [(D, KD), (1, 256)]), in_=wbf[sl].v(0, [(256, KD), (1, 256)])),
                    reads=[("wbf", sl)], writes=["wo"])
            return body
        run_stream([((lambda g=g: wo_loader(g)), wo_body(g)) for g in range(4)], pre=pre_b2)
        X1 = lambda b: A1F.v(b * D, [(1, D)])

        def post_norm_residual(pb0, res_fn, res_key, gt, gkey, tmp, out_ap, out_key, scol):
            P = ps[pb0 // 2]
            skey = ("stat", scol)
            sap = stat.v(scol, [(1, 1)])
            add("act", lambda e: e.activation(out=JUNK, in_=P.v(0, [(1, D)]), func=AF.Square, accum_out=sap),
                reads=[("ps", pb0), ("ps", pb0 + 1), skey], writes=["junk", skey])
            rstd_from(sap, 1.0 / D, skey)
            add("dve", lambda e: e.scalar_tensor_tensor(out=tmp, in0=P.v(0, [(1, D)]), scalar=sap, in1=gt, op0=ALU.mult, op1=ALU.mult),
                reads=[("ps", pb0), ("ps", pb0 + 1), skey, gkey], writes=["tmp"])
            add("dve", lambda e: e.tensor_tensor(out=out_ap, in0=tmp, in1=res_fn(), op=ALU.add),
                reads=["tmp", res_key], writes=[out_key])

        for b in range(NB):
            s = b % 2
            add("sp", lambda e, b=b, s=s: e.dma_start(out=XR[s], in_=x_h.ap()[b * 128:(b + 1) * 128, :]),
                writes=[("xr", s)], dma=("xr", s))
            pb0 = 4 + 2 * (b % 2)
            P = ps[pb0 // 2]
            t = b // 4
            for hcol in range(2):
                for k in range(KD):
                    mm(P.v(hcol * 512, [(1, 512)]), A3.v(k * S + b * 128, [(1, 128)]), WO(k, hcol * 512, 512),
                       k == 0, k == KD - 1, [("mT", k, t), "wo"], [("ps", pb0 + hcol)])
            post_norm_residual(pb0, lambda s=s: XR[s], ("xr", s), G2T, "g2t", TMB, X1(b), ("x1", b), NB + b)

        up_loader = lambda j: load_w([wrows(wu_h, 128 * j, 128), wrows(wu_h, DFF + 128 * j, 128)], KD, 256)
        pre_c = up_loader(0)
        Sd.barrier()

        HS = S // 2
        HB = NB // 2
        HT_ = HS // 512
        h2_off = 0
        ac_off = 8 * HS
        assert ac_off + KF * HS <= A2W
        TMC = A3F.v(0, [(1, D)])
        OT = [A3F.v(1024 + i * D, [(1, D)]) for i in range(2)]
        SL = A3F.v(3072, [(1, 512)])
        FF0 = lambda bb: A3F.v(3584 + bb * 512, [(1, 512)])
        outs = []
        for half in range(2):
            norm_transpose_seq([(None, (lambda b=half * HB + bb: X1(b)), ("x1", half * HB + bb), 2 * NB + half * HB + bb,
                                 A2.v(h2_off + bb * 128, [(HS, KD), (1, 128)]), ("h2T", bb // 4)) for bb in range(HB)], pbase=0)

            def up_body(j):
                def body(s):
                    for t in range(HT_):
                        p = (j * HT_ + t) % 3
                        bg_, bu_ = 2 * p, 2 * p + 1
                        Pg_, og_ = bank(bg_)
                        Pu_, ou_ = bank(bu_)
                        for k in range(KD):
                            mm(Pg_.v(og_, [(1, 512)]), wbf[s].v(k * 256, [(1, 128)]), A2.v(h2_off + k * HS + t * 512, [(1, 512)]),
                               k == 0, k == KD - 1, [("wbf", s), ("h2T", t)], [("ps", bg_)])
                        for k in range(KD):
                            mm(Pu_.v(ou_, [(1, 512)]), wbf[s].v(k * 256 + 128, [(1, 128)]), A2.v(h2_off + k * HS + t * 512, [(1, 512)]),
                               k == 0, k == KD - 1, [("wbf", s), ("h2T", t)], [("ps", bu_)])
                        add("act", lambda e, P=Pg_, o=og_: e.activation(out=SL, in_=P.v(o, [(1, 512)]), func=AF.Silu),
                            reads=[("ps", bg_)], writes=["sl"])
                        add("dve", lambda e, P=Pu_, o=ou_, t=t: e.tensor_tensor(out=A2.v(ac_off + j * HS + t * 512, [(1, 512)]), in0=P.v(o, [(1, 512)]),
                                                                                 in1=SL, op=ALU.mult),
                            reads=[("ps", bu_), "sl"], writes=[("actT", j, t)])
                return body

            def down_body(hc, j0, nj):
                def body(s):
                    for jj in range(nj):
                        j = j0 + jj
                        for bb in range(HB):
                            P, o = bank(bb)
                            mm(P.v(o, [(1, 512)]), A2.v(ac_off + j * HS + bb * 128, [(1, 128)]), wbf[s].v(jj * 512, [(1, 512)]),
                               j == 0, j == KF - 1, [("wbf", s), ("actT", j, bb // 4)], [("ps", bb)])
                    if j0 + nj == KF:
                        def sq(bb):
                            b = half * HB + bb
                            P, o = bank(bb)
                            sc = (3 + hc) * NB + b
                            add("act", lambda e: e.activation(out=A3.v(JUNK_O, [(1, 512)]), in_=P.v(o, [(1, 512)]), func=AF.Square,
                                                              accum_out=stat.v(sc, [(1, 1)])),
                                reads=[("ps", bb), ("stat", sc)], writes=["junk", ("stat", sc)])
                            if hc == 0:
                                add("dve", lambda e: e.tensor_tensor(out=FF0(bb), in0=P.v(o, [(1, 512)]),
                                                                     in1=A3F.v(9216, [(1, 512)]), op=ALU.mult),
                                    reads=[("ps", bb), "g4t"], writes=[("ff0", bb)])
                            else:
                                sa, sbk = ("stat", 3 * NB + b), ("stat", 4 * NB + b)
                                add("dve", lambda e: e.tensor_tensor(out=stat.v(4 * NB + b, [(1, 1)]), in0=stat.v(4 * NB + b, [(1, 1)]),
                                                                     in1=stat.v(3 * NB + b, [(1, 1)]), op=ALU.add),
                                    reads=[sa, sbk], writes=[sbk])

                        def norm_store(bb):
                            b = half * HB + bb
                            P, o = bank(bb)
                            so = b % 2
                            sbk = ("stat", 4 * NB + b)
                            sap = stat.v(4 * NB + b, [(1, 1)])
                            rstd_from(sap, 1.0 / D, sbk)
                            add("dve", lambda e: e.scalar_tensor_tensor(
                                out=A3F.v(1024 + so * D, [(1, 512)]), in0=FF0(bb), scalar=sap, in1=A1F.v(b * D, [(1, 512)]),
                                op0=ALU.mult, op1=ALU.add), reads=[("ff0", bb), sbk, ("x1", b)], writes=[("ot", so)])
                            add("dve", lambda e: e.scalar_tensor_tensor(
                                out=A3F.v(512, [(1, 512)]), in0=P.v(o, [(1, 512)]), scalar=sap, in1=A3F.v(9216 + 512, [(1, 512)]),
                                op0=ALU.mult, op1=ALU.mult), reads=[("ps", bb), sbk, "g4t"], writes=[("tmc", 1)])
                            add("dve", lambda e: e.tensor_tensor(
                                out=A3F.v(1024 + so * D + 512, [(1, 512)]), in0=A3F.v(512, [(1, 512)]),
                                in1=A1F.v(b * D + 512, [(1, 512)]), op=ALU.add),
                                reads=[("tmc", 1), ("x1", b)], writes=[("ot", so)])
                            outs.append(add("sp", lambda e: e.dma_start(out=out_h.ap()[b * 128:(b + 1) * 128, :], in_=OT[so]),
                                            reads=[("ot", so)], dma=("out", so)))

                        if hc == 0:
                            for bb in range(HB):
                                sq(bb)
                        else:
                            sq(0)
                            if HB > 1:
                                sq(1)
                            for bb in range(HB):
                                norm_store(bb)
                                if bb + 2 < HB:
                                    sq(bb + 2)
                return body

            items = [((lambda j=j: up_loader(j)), up_body(j)) for j in range(KF)]
            for hc in range(2):
                for j0 in range(0, KF, 4):
                    nj = min(4, KF - j0)
                    items.append(((lambda hc=hc, j0=j0, nj=nj: load_w([wrows(wd_h, hc * 512, 512)[:, j0:j0 + nj, :]], nj, 512)),
                                  down_body(hc, j0, nj)))
            run_stream(items, pre=(pre_c if half == 0 else None))
        Sd.emit(nc, final_waits=outs)
    return nc


def make_consts():
    c = np.zeros((128, 640), np.float32)
    c[:, 0:128] = np.eye(128, dtype=np.float32)
    j = np.arange(128)[:, None]
    s = np.arange(128)[None, :]
    c[:, 128:256] = np.where(j >= s, -1.0, 0.0)
    c[:, 256:384] = -1.0
    c[:, 384:512] = np.where(j >= s, NEG, 0.0)
    c[:, 512:640] = 1.0 / CONV
    return c


_NC_CACHE = {}


def kernel(**inputs):
    x = np.ascontiguousarray(np.asarray(inputs["x"], dtype=np.float32))
    B, S, _ = x.shape
    if S not in _NC_CACHE:
        _NC_CACHE[S] = build(S)
    nc = _NC_CACHE[S]
    shared = {k: np.ascontiguousarray(np.asarray(v, dtype=np.float32)) for k, v in inputs.items() if k != "x"}
    shared["consts"] = make_consts()
    in_maps = []
    for b in range(B):
        m = dict(shared)
        m["x"] = x[b]
        in_maps.append(m)
    res = run_bass_kernel_spmd(nc, in_maps, core_ids=list(range(B)))
    return np.stack([np.asarray(r["out"]) for r in res.results], axis=0).astype(np.float32)
```
